# Optimizing a Trainium2 kernel written in Bass

```python
import math
import jax, jax.numpy as jnp
from jax import lax
import numpy as np

D_MODEL = 1024
BATCH = 32
SEQ = 2048
DEPTH = 4

CTX_LEN = 256
GRID_W = 64
N_MIXERS = 3
ROPE_BASE = 10000.0
EPS = 1e-6
BLOCK = 128

A_HEADS = 16
A_KV_HEADS = 4
A_GROUP = A_HEADS // A_KV_HEADS
A_HEAD_DIM = D_MODEL // A_HEADS
A_WIDTH = A_HEADS * A_HEAD_DIM
A_KV_WIDTH = A_KV_HEADS * A_HEAD_DIM
A_IN = 2 * A_WIDTH + 2 * A_KV_WIDTH
WINDOW = 128

B_HEADS = 16
B_NOPE = 64
B_ROPE = 32
B_V = 64
B_Q_LORA = D_MODEL // 2
B_KV_LORA = D_MODEL // 4
B_WIDTH = B_HEADS * B_V
B_IN = B_Q_LORA + B_KV_LORA + B_ROPE + B_WIDTH

C_INNER = 2 * D_MODEL
C_HEAD_DIM = 64
C_HEADS = C_INNER // C_HEAD_DIM
C_GROUPS = 4
C_HPG = C_HEADS // C_GROUPS
C_STATE = 128
C_CONV_K = 3
C_CHUNK = 128
C_CONV_DIM = C_INNER + 2 * C_GROUPS * C_STATE
C_IN = C_INNER + C_CONV_DIM + 2 * C_HEADS
DT_MIN = 1e-3
DT_MAX = 1e-1

kernel_name = "hybrid_interleaved_swa_mla_ssd_prefix_ctx"

F32 = jnp.float32


def rms_norm(u, g):
    uf = u.astype(F32)
    y = uf * lax.rsqrt(jnp.mean(uf * uf, axis=-1, keepdims=True) + EPS)
    return (y * g.astype(F32)).astype(u.dtype)


def adaln(cond, w, bias):
    m = jax.nn.silu(cond) @ w + bias
    return jnp.split(m, 3, axis=-1)


def axial_rope(rows, dim, dtype):
    row = jnp.repeat(jnp.arange(rows), GRID_W).astype(F32)
    col = (jnp.arange(rows * GRID_W) % GRID_W).astype(F32)
    nf = dim // 4
    inv = ROPE_BASE ** (-jnp.arange(nf, dtype=F32) / nf)
    ar = row[:, None] * inv
    ac = col[:, None] * inv
    ang = jnp.concatenate([ar, ar, ac, ac], axis=-1)
    return jnp.cos(ang).astype(dtype), jnp.sin(ang).astype(dtype)


def apply_rope(u, cos, sin):
    d = u.shape[-1]
    shape = (cos.shape[0],) + (1,) * (u.ndim - 3) + (d,)
    r1, r2, c1, c2 = jnp.split(u, 4, axis=-1)
    rot = jnp.concatenate([-r2, r1, -c2, c1], axis=-1)
    return u * cos.reshape(shape) + rot * sin.reshape(shape)


def softmax_with_sink(s, sink):
    sk = jnp.broadcast_to(sink, s.shape[:-1] + (1,))
    p = jax.nn.softmax(jnp.concatenate([s, sk], axis=-1), axis=-1)
    return p[..., :-1]


def band_blocks(u, nb):
    pad = [(0, 0), (BLOCK, BLOCK)] + [(0, 0)] * (u.ndim - 2)
    up = jnp.pad(u, pad).reshape((u.shape[0], nb + 2, BLOCK) + u.shape[2:])
    return jnp.concatenate([up[:, :-2], up[:, 1:-1], up[:, 2:]], axis=2)


def window_gqa(hx, hc, w_in, sink, w_out, cos, sin, need_ctx):
    b, T, _ = hx.shape
    scale = A_HEAD_DIM ** -0.5
    sink_b = sink.astype(F32).reshape(A_KV_HEADS, A_GROUP)[None, :, :, None, None]

    def project(h):
        n = h.shape[1]
        q, k, v, g = jnp.split(h @ w_in, [A_WIDTH, A_WIDTH + A_KV_WIDTH, A_WIDTH + 2 * A_KV_WIDTH], axis=-1)
        return (q.reshape(b, n, A_KV_HEADS, A_GROUP, A_HEAD_DIM),
                k.reshape(b, n, A_KV_HEADS, A_HEAD_DIM),
                v.reshape(b, n, A_KV_HEADS, A_HEAD_DIM), g)

    qx, kx, vx, gx = project(hx)
    qc, kc, vc, gc = project(hc)
    qx = apply_rope(qx, cos, sin)
    kx = apply_rope(kx, cos, sin)

    nb = T // BLOCK
    qi = jnp.arange(BLOCK)[:, None]
    kj = jnp.arange(3 * BLOCK)[None, :]
    near = jnp.abs(kj - BLOCK - qi) <= WINDOW
    kpos = jnp.arange(nb)[:, None] * BLOCK - BLOCK + jnp.arange(3 * BLOCK)[None, :]
    inside = (kpos >= 0) & (kpos < T)
    mask = near[None] & inside[:, None, :]

    qb = jnp.moveaxis(qx.reshape(b, nb, BLOCK, A_KV_HEADS, A_GROUP, A_HEAD_DIM), 1, 0)
    kw = jnp.moveaxis(band_blocks(kx, nb), 1, 0)
    vw = jnp.moveaxis(band_blocks(vx, nb), 1, 0)

    def attend_block(args):
        q, k, v, m = args
        s_loc = jnp.einsum("bqhgd,bkhd->bhgqk", q, k).astype(F32) * scale
        s_loc = jnp.where(m, s_loc, -jnp.inf)
        s_ctx = jnp.einsum("bqhgd,bchd->bhgqc", q, kc).astype(F32) * scale
        p = softmax_with_sink(jnp.concatenate([s_loc, s_ctx], axis=-1), sink_b).astype(v.dtype)
        return (jnp.einsum("bhgqk,bkhd->bqhgd", p[..., :3 * BLOCK], v)
                + jnp.einsum("bhgqc,bchd->bqhgd", p[..., 3 * BLOCK:], vc))

    ox = lax.map(attend_block, (qb, kw, vw, mask))
    ox = jnp.moveaxis(ox, 0, 1).reshape(b, T, A_WIDTH)
    yx = (ox * jax.nn.silu(gx)) @ w_out
    yc = None
    if need_ctx:
        sc = jnp.einsum("bqhgd,bkhd->bhgqk", qc, kc).astype(F32) * scale
        pc = softmax_with_sink(sc, sink_b).astype(vc.dtype)
        oc = jnp.einsum("bhgqk,bkhd->bqhgd", pc, vc).reshape(b, hc.shape[1], A_WIDTH)
        yc = (oc * jax.nn.silu(gc)) @ w_out
    return yx, yc


def latent_attention(hx, hc, w_in, q_norm, w_uq, kv_norm, w_ukv, w_out, cos, sin, need_ctx):
    b, T, _ = hx.shape
    scale = (B_NOPE + B_ROPE) ** -0.5

    def project(h):
        n = h.shape[1]
        cq, ckv, kr, g = jnp.split(h @ w_in, [B_Q_LORA, B_Q_LORA + B_KV_LORA, B_Q_LORA + B_KV_LORA + B_ROPE], axis=-1)
        q = (rms_norm(cq, q_norm) @ w_uq).reshape(b, n, B_HEADS, B_NOPE + B_ROPE)
        kv = (rms_norm(ckv, kv_norm) @ w_ukv).reshape(b, n, B_HEADS, B_NOPE + B_V)
        return q[..., :B_NOPE], q[..., B_NOPE:], kv[..., :B_NOPE], kr, kv[..., B_NOPE:], g

    qnx, qrx, knx, krx, vx, gx = project(hx)
    qnc, qrc, knc, krc, vc, gc = project(hc)
    qrx = apply_rope(qrx, cos, sin)
    krx = apply_rope(krx, cos, sin)

    def attend(qn, qr, kn, kr, v):
        s = (jnp.einsum("bqhd,bkhd->bhqk", qn, kn)
             + jnp.einsum("bqhd,bkd->bhqk", qr, kr)).astype(F32) * scale
        p = jax.nn.softmax(s, axis=-1).astype(v.dtype)
        return jnp.einsum("bhqk,bkhd->bqhd", p, v)

    kn_all = jnp.concatenate([knc, knx], axis=1)
    kr_all = jnp.concatenate([krc, krx], axis=1)
    v_all = jnp.concatenate([vc, vx], axis=1)
    nb = T // BLOCK
    to_blocks = lambda u: jnp.moveaxis(u.reshape((b, nb, BLOCK) + u.shape[2:]), 1, 0)
    ox = lax.map(lambda qq: attend(qq[0], qq[1], kn_all, kr_all, v_all), (to_blocks(qnx), to_blocks(qrx)))
    ox = jnp.moveaxis(ox, 0, 1).reshape(b, T, B_WIDTH)
    yx = (ox * jax.nn.silu(gx)) @ w_out
    yc = None
    if need_ctx:
        oc = attend(qnc, qrc, knc, krc, vc).reshape(b, hc.shape[1], B_WIDTH)
        yc = (oc * jax.nn.silu(gc)) @ w_out
    return yx, yc


def depthwise_conv(u, w, bias):
    pad = C_CONV_K // 2
    out = lax.conv_general_dilated(u, w[:, None, :].astype(u.dtype), (1,), [(pad, pad)],
                                   dimension_numbers=("NWC", "WIO", "NWC"),
                                   feature_group_count=u.shape[-1])
    return out + bias


def segsum(a):
    cs = jnp.cumsum(a, axis=-1)
    diff = cs[..., :, None] - cs[..., None, :]
    L = a.shape[-1]
    return jnp.where(jnp.tril(jnp.ones((L, L), bool)), diff, -jnp.inf)


def ssd_chunked(X, dtA, Bm, Cm, init):
    b, n, G, R, P = X.shape
    nc = n // C_CHUNK
    Xc = X.astype(F32).reshape(b, nc, C_CHUNK, G, R, P)
    Bc = Bm.astype(F32).reshape(b, nc, C_CHUNK, G, C_STATE)
    Cc = Cm.astype(F32).reshape(b, nc, C_CHUNK, G, C_STATE)
    Ac = dtA.astype(F32).reshape(b, nc, C_CHUNK, G, R).transpose(0, 3, 4, 1, 2)
    A_cum = jnp.cumsum(Ac, axis=-1)
    Lmat = jnp.exp(segsum(Ac))
    CB = jnp.einsum("bclgn,bcsgn->bgcls", Cc, Bc)
    y_diag = jnp.einsum("bgcls,bgrcls,bcsgrp->bclgrp", CB, Lmat, Xc)
    decay_states = jnp.exp(A_cum[..., -1:] - A_cum)
    states = jnp.einsum("bclgn,bgrcl,bclgrp->bcgrpn", Bc, decay_states, Xc)
    states = jnp.concatenate([init[:, None], states], axis=1)
    chunk_decay = jnp.exp(segsum(jnp.pad(A_cum[..., -1], ((0, 0), (0, 0), (0, 0), (1, 0)))))
    new_states = jnp.einsum("bgrzc,bcgrpn->bzgrpn", chunk_decay, states)
    y_off = jnp.einsum("bclgn,bcgrpn,bgrcl->bclgrp", Cc, new_states[:, :-1], jnp.exp(A_cum))
    return (y_diag + y_off).reshape(b, n, G, R, P), new_states[:, -1]


def bidirectional_ssd(hx, hc, w_in, conv_w, conv_b, dt_bias, a_log, d_skip, norm_w, w_out, need_ctx):
    b = hx.shape[0]
    A = -jnp.exp(a_log.astype(F32))

    def prep(h):
        n = h.shape[1]
        z, xbc, dt = jnp.split(h @ w_in, [C_INNER, C_INNER + C_CONV_DIM], axis=-1)
        xbc = jax.nn.silu(depthwise_conv(xbc, conv_w, conv_b))
        xs, Bm, Cm = jnp.split(xbc, [C_INNER, C_INNER + C_GROUPS * C_STATE], axis=-1)
        dt = jax.nn.softplus(dt.astype(F32).reshape(b, n, 2, C_HEADS) + dt_bias.astype(F32))
        return (z, xs.reshape(b, n, C_GROUPS, C_HPG, C_HEAD_DIM),
                Bm.reshape(b, n, C_GROUPS, C_STATE), Cm.reshape(b, n, C_GROUPS, C_STATE), dt)

    def scan_dir(d, xs, Bm, Cm, dt, init):
        dtd = dt[:, :, d].reshape(b, xs.shape[1], C_GROUPS, C_HPG)
        return ssd_chunked(xs.astype(F32) * dtd[..., None], dtd * A[d].reshape(C_GROUPS, C_HPG), Bm, Cm, init)

    flip = lambda u: jnp.flip(u, axis=1)
    zx, xx, Bx, Cx, dtx = prep(hx)
    zc, xc, Bc, Cc, dtc = prep(hc)
    init = jnp.zeros((b, C_GROUPS, C_HPG, C_HEAD_DIM, C_STATE), F32)
    yf_c, s_f = scan_dir(0, xc, Bc, Cc, dtc, init)
    yf_x, _ = scan_dir(0, xx, Bx, Cx, dtx, s_f)
    yb_c, s_b = scan_dir(1, flip(xc), flip(Bc), flip(Cc), flip(dtc), init)
    yb_x, _ = scan_dir(1, flip(xx), flip(Bx), flip(Cx), flip(dtx), s_b)
    dsk = d_skip.astype(F32).reshape(C_GROUPS, C_HPG, 1)

    def finish(yf, yb, xs, z):
        n = xs.shape[1]
        y = (yf + flip(yb) + dsk * xs.astype(F32)).reshape(b, n, C_INNER)
        y = y * jax.nn.silu(z.astype(F32))
        yg = y.reshape(b, n, C_GROUPS, C_INNER // C_GROUPS)
        yg = yg * lax.rsqrt(jnp.mean(yg * yg, axis=-1, keepdims=True) + EPS)
        y = (yg.reshape(b, n, C_INNER) * norm_w.astype(F32)).astype(xs.dtype)
        return y @ w_out

    yx = finish(yf_x, yb_x, xx, zx)
    yc = finish(yf_c, yb_c, xc, zc) if need_ctx else None
    return yx, yc


def setup_inputs(seed: int = 0) -> dict:
    key = jax.random.key(seed)
    ks = iter(jax.random.split(key, 40))

    def nrm(shape, scale):
        return jax.random.normal(next(ks), shape, jnp.float32) * scale

    def gain(shape):
        return 1.0 + nrm(shape, 0.05)

    D = D_MODEL
    n_a, n_b, n_c = (len(range(k, DEPTH, N_MIXERS)) for k in range(N_MIXERS))
    dt_cols = jnp.where(jnp.arange(C_IN) >= C_INNER + C_CONV_DIM, 0.1, 1.0).astype(jnp.float32)
    dt0 = jnp.exp(jax.random.uniform(next(ks), (n_c, 2, C_HEADS), jnp.float32, math.log(DT_MIN), math.log(DT_MAX)))
    a_init = jax.random.uniform(next(ks), (n_c, 2, C_HEADS), jnp.float32, 1.0, 16.0)
    return {
        "x": nrm((BATCH, SEQ, D), 1.0),
        "c": nrm((BATCH, D), 1.0),
        "ctx": nrm((BATCH, CTX_LEN, D), 1.0),
        "c_ctx": nrm((D,), 1.0),
        "ada_w": nrm((DEPTH, D, 3 * D), 0.5 * D ** -0.5),
        "ada_b": nrm((DEPTH, 3 * D), 0.02),
        "norm_g": gain((DEPTH, D)),
        "final_g": gain((D,)),
        "a_w_in": nrm((n_a, D, A_IN), D ** -0.5),
        "a_sink": nrm((n_a, A_HEADS), 1.0),
        "a_w_out": nrm((n_a, A_WIDTH, D), A_WIDTH ** -0.5),
        "b_w_in": nrm((n_b, D, B_IN), D ** -0.5),
        "b_q_norm": gain((n_b, B_Q_LORA)),
        "b_w_uq": nrm((n_b, B_Q_LORA, B_HEADS * (B_NOPE + B_ROPE)), B_Q_LORA ** -0.5),
        "b_kv_norm": gain((n_b, B_KV_LORA)),
        "b_w_ukv": nrm((n_b, B_KV_LORA, B_HEADS * (B_NOPE + B_V)), B_KV_LORA ** -0.5),
        "b_w_out": nrm((n_b, B_WIDTH, D), B_WIDTH ** -0.5),
        "c_w_in": nrm((n_c, D, C_IN), D ** -0.5) * dt_cols,
        "c_conv_w": nrm((n_c, C_CONV_K, C_CONV_DIM), C_CONV_K ** -0.5),
        "c_conv_b": nrm((n_c, C_CONV_DIM), 0.02),
        "c_dt_bias": dt0 + jnp.log(-jnp.expm1(-dt0)),
        "c_a_log": jnp.log(a_init),
        "c_d": 1.0 + nrm((n_c, C_HEADS), 0.1),
        "c_norm": gain((n_c, C_INNER)),
        "c_w_out": nrm((n_c, C_INNER, D), C_INNER ** -0.5),
    }


def reference(x, c, ctx, c_ctx, ada_w, ada_b, norm_g, final_g,
              a_w_in, a_sink, a_w_out,
              b_w_in, b_q_norm, b_w_uq, b_kv_norm, b_w_ukv, b_w_out,
              c_w_in, c_conv_w, c_conv_b, c_dt_bias, c_a_log, c_d, c_norm, c_w_out):
    T = x.shape[1]
    ROWS = T // GRID_W
    cos_a, sin_a = axial_rope(ROWS, A_HEAD_DIM, x.dtype)
    cos_b, sin_b = axial_rope(ROWS, B_ROPE, x.dtype)
    for i in range(DEPTH):
        kind = i % N_MIXERS
        j = i // N_MIXERS
        need_ctx = i < DEPTH - 1
        shift_x, scale_x, gate_x = adaln(c, ada_w[i], ada_b[i])
        shift_c, scale_c, gate_c = adaln(c_ctx, ada_w[i], ada_b[i])
        hx = rms_norm(x, norm_g[i]) * (1.0 + scale_x[:, None]) + shift_x[:, None]
        hc = rms_norm(ctx, norm_g[i]) * (1.0 + scale_c) + shift_c
        if kind == 0:
            yx, yc = window_gqa(hx, hc, a_w_in[j], a_sink[j], a_w_out[j], cos_a, sin_a, need_ctx)
        elif kind == 1:
            yx, yc = latent_attention(hx, hc, b_w_in[j], b_q_norm[j], b_w_uq[j], b_kv_norm[j],
                                      b_w_ukv[j], b_w_out[j], cos_b, sin_b, need_ctx)
        else:
            yx, yc = bidirectional_ssd(hx, hc, c_w_in[j], c_conv_w[j], c_conv_b[j], c_dt_bias[j],
                                       c_a_log[j], c_d[j], c_norm[j], c_w_out[j], need_ctx)
        x = x + gate_x[:, None] * yx
        if need_ctx:
            ctx = ctx + gate_c * yc
    return rms_norm(x, final_g)
```

```python
from contextlib import ExitStack
import math
import numpy as np
import concourse.bass as bass
import concourse.mybir as mybir
from concourse.bass_utils import run_bass_kernel_spmd

F32 = mybir.dt.float32
BF16 = mybir.dt.bfloat16
AF = mybir.ActivationFunctionType
ALU = mybir.AluOpType
AX = mybir.AxisListType

ENGS = ("pe", "act", "dve", "pool", "sp")
N_CORES = 8
D = 1024
NX = 2048
NCTX = 256
NT = NX + NCTX
NTILE = NT // 128
EPS = 1e-6
DEPTH = 4


class Op:
    __slots__ = ("eng", "fn", "deps", "seq", "inc", "count", "chan", "waits", "known", "is_dma")


class Prog:
    def __init__(self, nc, stack):
        self.nc = nc
        self.stack = stack
        self.ops = {e: [] for e in ENGS}
        self.allops = []
        self.last_w = {}
        self.readers = {}
        self.chan_ops = {}
        self.sems = {}

    def sbuf(self, name, shape, dt):
        return self.stack.enter_context(self.nc.sbuf_tensor(name, list(shape), dt))

    def psum(self, name, shape, dt):
        return self.stack.enter_context(self.nc.psum_tensor(name, list(shape), dt))

    def add(self, eng, fn, r=(), w=(), dma=None, extra_deps=None):
        o = Op()
        o.eng = eng
        o.fn = fn
        o.is_dma = dma is not None
        o.chan = ("dma", dma) if o.is_dma else eng
        deps = {}
        for k in r:
            x = self.last_w.get(k)
            if x is not None:
                deps[id(x)] = (x, True)
            if isinstance(k, tuple) and k[0] == "ps":
                for x in self.readers.get(k, ()):
                    if x.eng != eng and id(x) not in deps:
                        deps[id(x)] = (x, False)
        for k in w:
            x = self.last_w.get(k)
            if x is not None and id(x) not in deps:
                deps[id(x)] = (x, False)
            for x in self.readers.get(k, ()):
                if id(x) not in deps:
                    deps[id(x)] = (x, False)
        dl = []
        for x, raw in deps.values():
            if (not x.is_dma) and (not o.is_dma) and x.eng == eng and not raw and eng == "pe":
                continue
            dl.append(x)
        if extra_deps:
            dl.extend(extra_deps)
        o.deps = dl
        o.inc = False
        for k in r:
            self.readers.setdefault(k, []).append(o)
        for k in w:
            self.last_w[k] = o
            self.readers[k] = []
        self.ops[eng].append(o)
        self.allops.append(o)
        self.chan_ops.setdefault(o.chan, []).append(o)
        o.seq = len(self.chan_ops[o.chan])
        return o

    def dma(self, eng, out, in_, r=(), w=(), sem=None, **kw):
        return self.add(eng, lambda e: e.dma_start(out=out, in_=in_, **kw), r=r, w=w, dma=sem)

    def barrier(self):
        lasts = [lst[-1] for lst in self.chan_ops.values() if lst]
        for e in ENGS:
            self.add(e, lambda eng: eng.nop(), extra_deps=list(lasts))
        self.last_w = {}
        self.readers = {}

    def emit(self):
        nc = self.nc
        known = {e: {} for e in ENGS}
        for o in self.allops:
            kn = known[o.eng]
            waits = []
            for x in sorted(o.deps, key=lambda x: -x.seq):
                if kn.get(x.chan, 0) >= x.seq:
                    continue
                waits.append(x)
                x.inc = True
                kn[x.chan] = x.seq
                for c, v in x.known.items():
                    if kn.get(c, 0) < v:
                        kn[c] = v
            o.waits = waits
            o.known = dict(kn)
        for chan, lst in self.chan_ops.items():
            c = 0
            for o in lst:
                if o.is_dma:
                    o.inc = True
                if o.inc:
                    c += 16 if o.is_dma else 1
                o.count = c
        for chan in self.chan_ops:
            nm = "s_" + "".join(ch if ch.isalnum() else "_" for ch in str(chan))
            self.sems[chan] = self.stack.enter_context(nc.semaphore(nm))
        engmap = {"pe": "tensor", "act": "scalar", "dve": "vector", "pool": "gpsimd", "sp": "sync"}
        self.n_waits = 0
        with nc.Block() as block:
            for e in ENGS:
                ops = self.ops[e]
                if not ops:
                    continue

                def body(eng, ops=ops):
                    for o in ops:
                        for x in o.waits:
                            eng.wait_ge(self.sems[x.chan], x.count)
                            self.n_waits += 1
                        ins = o.fn(eng)
                        if o.inc:
                            ins.then_inc(self.sems[o.chan], 16 if o.is_dma else 1)

                getattr(block, engmap[e])(body)


def emit_skewed(jobs, skew):
    n = len(jobs)
    for i in range(n + skew):
        if i < n:
            jobs[i][0]()
        if i - skew >= 0:
            jobs[i - skew][1]()


class Rot:
    def __init__(self, items):
        self.items = items
        self.i = 0

    def next(self):
        it = self.items[self.i % len(self.items)]
        self.i += 1
        return it


def rope_tables(dim, reps):
    nf = dim // 4
    t = np.arange(NX)
    row = (t // 64).astype(np.float32)
    col = (t % 64).astype(np.float32)
    inv = (10000.0 ** (-np.arange(nf, dtype=np.float32) / nf)).astype(np.float32)
    ar = row[:, None] * inv
    ac = col[:, None] * inv
    ang = np.concatenate([ar, ar, ac, ac], axis=-1).astype(np.float32)
    cos = np.cos(ang).astype(np.float32).T
    sin = np.sin(ang).astype(np.float32).T
    sp = np.zeros((dim, dim), np.float32)
    for i in range(dim):
        q = i // nf
        if q % 2 == 0:
            sp[i + nf, i] = -1.0
        else:
            sp[i - nf, i] = 1.0
    cosr = np.tile(cos, (reps, 1))
    sinr = np.tile(sin, (reps, 1))
    spr = np.kron(np.eye(reps, dtype=np.float32), sp)
    return np.ascontiguousarray(cosr), np.ascontiguousarray(sinr), np.ascontiguousarray(spr)


def gqa_perms():
    qperm = np.zeros(1024, np.int64)
    for jj in range(2):
        for g in range(4):
            for half in range(2):
                head = (2 * jj + half) * 4 + g
                cq = jj * 4 + g
                qperm[cq * 128 + half * 64: cq * 128 + half * 64 + 64] = head * 64 + np.arange(64)
    operm = np.zeros(1024, np.int64)
    for kvh in range(4):
        for gp in range(2):
            for half in range(2):
                head = kvh * 4 + gp + 2 * half
                c = kvh * 2 + gp
                operm[c * 128 + half * 64: c * 128 + half * 64 + 64] = head * 64 + np.arange(64)
    return qperm, operm


def build_program(layers, nseq, final):
    nc = bass.Bass("TRN2", target_bir_lowering=False)
    NL = len(layers)

    def din(name, shape):
        return nc.dram_tensor(name, list(shape), F32, kind="ExternalInput").ap()

    def dout(name, shape):
        return nc.dram_tensor(name, list(shape), F32, kind="ExternalOutput").ap()

    x_d = din("x", [nseq, NX, D])
    ctx_d = din("ctx", [nseq, NCTX, D])
    cond_d = din("condT", [128, 8, nseq + 1])
    adaw_d = din("ada_w", [DEPTH, D, 3 * D])
    adab_d = din("ada_bT", [DEPTH, 128, 24])
    ng_d = din("norm_gT", [DEPTH, 128, 8])
    fg_d = din("final_gT", [128, 8])
    ident_d = din("ident", [128, 128])
    ones_d = din("ones", [128, 128])
    has_a = any(l % 3 == 0 for l in layers)
    has_b = any(l % 3 == 1 for l in layers)
    has_c = any(l % 3 == 2 for l in layers)
    if has_a:
        a_win_d = din("a_w_in", [2, D, 2560])
        a_sink_d = din("a_sink_bc", [2, 128, 16])
        a_wout_d = din("a_w_out", [2, D, D])
        ropeA_cos_d = din("ropeA_cos", [128, NX])
        ropeA_sin_d = din("ropeA_sin", [128, NX])
        spA_d = din("spA", [128, 128])
        mprev_d = din("mask_prev", [128, 128])
        mnext_d = din("mask_next", [128, 128])
    if has_b:
        b_win_d = din("b_w_in", [1, D, 1824])
        b_wuq_d = din("b_w_uq", [1, 512, 1536])
        b_wukv_d = din("b_w_ukv", [1, 256, 2048])
        b_wout_d = din("b_w_out", [1, D, D])
        b_qn_d = din("b_q_normT", [1, 128, 4])
        b_kvn_d = din("b_kv_normT", [1, 128, 2])
        ropeB_cos_d = din("ropeB_cos", [128, NX])
        ropeB_sin_d = din("ropeB_sin", [128, NX])
        spB_d = din("spB", [128, 128])
    if has_c:
        c_win_d = din("c_w_in", [1, D, 5184])
        c_wout_d = din("c_w_out", [1, 2048, D])
        c_cwT_d = din("c_conv_wT", [128, 24, 3])
        c_cbT_d = din("c_conv_bT", [128, 24])
        c_dtb_d = din("c_dt_bias_bc", [128, 64])
        c_alog_d = din("c_a_log_bc", [128, 64])
        c_dsk_d = din("c_d_bc", [128, 32])
        c_nwT_d = din("c_normT", [128, 16])
        triU_d = din("triU", [128, 128])
        triL_d = din("triL", [128, 128])
        triSL_d = din("triSL", [128, 128])
        triSU_d = din("triSU", [128, 128])
    if final:
        out_d = dout("out", [nseq, NX, D])
    else:
        xo_d = dout("x_out", [nseq, NX, D])
        co_d = dout("ctx_out", [nseq, NCTX, D])

    st = ExitStack()
    with st:
        P = Prog(nc, st)
        R = P.sbuf("R", [128, 8, NT], F32)
        hT = P.sbuf("hT", [128, 8, NT], BF16)
        mod = P.sbuf("mod", [128, NL, 24, nseq + 1], F32)
        Acol = P.sbuf("Acol", [128, 8, 2], F32)
        ident_f = P.sbuf("ident_f", [128, 128], F32)
        ident_b = P.sbuf("ident_b", [128, 128], BF16)
        ones_b = P.sbuf("ones_b", [128, 128], BF16)
        ngT = P.sbuf("ngT", [128, DEPTH, 8], F32)
        fgT = P.sbuf("fgT", [128, 8], F32)
        adabT = P.sbuf("adabT", [128, DEPTH, 24], F32)
        condT = P.sbuf("condT_sb", [128, 8, nseq + 1], F32)
        scond = P.sbuf("scond", [128, 8, nseq + 1], BF16)
        WS = [P.sbuf(f"WS{i}", [128, 8, 512], BF16) for i in range(2)]
        tmpf = [P.sbuf(f"tmpf{i}", [128, 512], F32) for i in range(3)]
        tmpb = [P.sbuf(f"tmpb{i}", [128, 512], BF16) for i in range(3)]
        rstd = P.sbuf("rstd", [128, 512], F32)
        ps = [P.psum(f"ps{i}", [128, 512], F32) for i in range(8)]

        arena_box = []

        def carve_bf(off_bytes, nelem):
            assert off_bytes + nelem * 2 <= ARENA_BYTES[0], (off_bytes, nelem, ARENA_BYTES[0])
            return arena_box[0][:, off_bytes // 4: off_bytes // 4 + nelem // 2].bitcast(BF16)

        def carve_f(off_bytes, nelem):
            assert off_bytes + nelem * 4 <= ARENA_BYTES[0], (off_bytes, nelem, ARENA_BYTES[0])
            return arena_box[0][:, off_bytes // 4: off_bytes // 4 + nelem]

        ARENA_BYTES = [0]
        STAGE_OFF = 16 * 1024

        ws_rot = Rot([(WS[0], ("ws", 0)), (WS[1], ("ws", 1))])
        tmpf_rot = Rot([(tmpf[i], ("tmpf", i)) for i in range(3)])
        tmpb_rot = Rot([(tmpb[i], ("tmpb", i)) for i in range(3)])

        def tiles_of(tok0, ntok):
            return range(tok0 // 128, (tok0 + ntok) // 128)

        XBLK = [(NCTX + 512 * j, 512) for j in range(4)]
        CBLK = (0, NCTX)

        def MM(out, lhsT, rhs, start, stop, r, w, **kw):
            P.add("pe", lambda e: e.matmul(out, lhsT=lhsT, rhs=rhs, start=start, stop=stop, **kw), r, w)

        def ACT(out, in_, func, r, w, **kw):
            P.add("act", lambda e: e.activation(out=out, in_=in_, func=func, **kw), r, w)

        def TT(out, in0, in1, op, r, w, eng="dve"):
            P.add(eng, lambda e: e.tensor_tensor(out=out, in0=in0, in1=in1, op=op), r, w)

        def TS(out, in0, s1, s2, op0, op1, r, w, eng="dve"):
            if s2 is None:
                P.add(eng, lambda e: e.tensor_scalar(out=out, in0=in0, scalar1=s1, scalar2=None, op0=op0), r, w)
            else:
                P.add(eng, lambda e: e.tensor_scalar(out=out, in0=in0, scalar1=s1, scalar2=s2, op0=op0, op1=op1), r, w)

        def STT(out, in0, scalar, in1, op0, op1, r, w):
            P.add("dve", lambda e: e.scalar_tensor_tensor(out=out, in0=in0, scalar=scalar, in1=in1, op0=op0, op1=op1), r, w)

        def CP(out, in_, r, w, eng="dve"):
            P.add(eng, lambda e: e.tensor_copy(out=out, in_=in_), r, w)

        def RECIP(out, in_, r, w):
            P.add("dve", lambda e: e.reciprocal(out=out, in_=in_), r, w)

        dma_ctr = [0]

        def DMA(eng, out, in_, r, w, sem):
            P.dma(eng, out, in_, r=r, w=w, sem=sem)

        def load_w(dst, dst_key, src_ap, sem):
            DMA("pool", dst, src_ap, r=[], w=[dst_key], sem=sem)

        DMA("sp", ident_f[:], ident_d, [], ["ident_f"], "c_identf")
        DMA("pool", ident_b[:], ident_d, [], ["ident_b"], "c_identb")
        DMA("pool", ones_b[:], ones_d, [], ["ones_b"], "c_onesb")
        DMA("sp", ngT[:], ng_d.rearrange("l p c -> p l c"), [], ["ngT"], "c_ng")
        DMA("sp", fgT[:], fg_d, [], ["fgT"], "c_fg")
        DMA("sp", adabT[:], adab_d.rearrange("l p c -> p l c"), [], ["adabT"], "c_adab")
        DMA("sp", condT[:], cond_d, [], ["condT"], "c_cond")
        ACT(scond[:], condT[:], AF.Silu, ["condT"], ["scond"])

        for li, l in enumerate(layers):
            for cg in range(6):
                wsb, wkey = ws_rot.next()
                load_w(wsb[:], wkey, adaw_d[l, :, cg * 512:(cg + 1) * 512].rearrange("(kc p) c -> p kc c", p=128),
                       sem=("ws", wkey[1]))
                for cc in range(4):
                    j = cg * 4 + cc
                    pst = ps[j % 2]
                    for kc in range(8):
                        MM(pst[:, 0:nseq + 1], wsb[:, kc, cc * 128:(cc + 1) * 128], scond[:, kc, :],
                           kc == 0, kc == 7, [wkey, "scond"], [("ps", j % 2)])
                    TS(mod[:, li, j, :], pst[:, 0:nseq + 1], adabT[:, l, j:j + 1], None, ALU.add, None,
                       [("ps", j % 2), "adabT"], [("mod", li)])

        def load_seq(s):
            for t in range(NTILE):
                stg, skey = stage_rot.next()
                src = ctx_d[s, t * 128:(t + 1) * 128, :] if t < 2 else x_d[s, (t - 2) * 128:(t - 1) * 128, :]
                DMA("sp", stg[:], src, [], [skey], ("stage", skey[1]))
                for half in range(2):
                    pst = ps[(2 * t + half) % 8]
                    pkey = ("ps", (2 * t + half) % 8)
                    for q in range(4):
                        dc = half * 4 + q
                        P.add("pe", lambda e, o=pst[:, q * 128:(q + 1) * 128], i=stg[:, dc * 128:(dc + 1) * 128]:
                              e.transpose(o, i, ident_f[:]), [skey, "ident_f"], [pkey])
                    eng = "dve" if half == 0 else "act"
                    outap = R[:, half * 4:half * 4 + 4, t * 128:(t + 1) * 128]
                    inap = pst[:].rearrange("p (a b) -> p a b", b=128)
                    if eng == "dve":
                        CP(outap, inap, [pkey], [("R", dc, t) for dc in range(half * 4, half * 4 + 4)])
                    else:
                        ACT(outap, inap, AF.Copy, [pkey], [("R", dc, t) for dc in range(half * 4, half * 4 + 4)])

        def store_seq(s, normed, tiles=None, fin_t0=0):
            if tiles is None:
                tiles = range(NTILE)
            for t in tiles:
                stg, skey = stage_rot.next()
                for half in range(2):
                    pst = ps[(2 * t + half) % 8]
                    pkey = ("ps", (2 * t + half) % 8)
                    for q in range(4):
                        dc = half * 4 + q
                        src = normed[:, dc, (t - fin_t0) * 128:(t - fin_t0 + 1) * 128] if final else R[:, dc, t * 128:(t + 1) * 128]
                        rk = [("fin", dc, t)] if final else [("R", dc, t)]
                        P.add("pe", lambda e, o=pst[:, q * 128:(q + 1) * 128], i=src: e.transpose(o, i, ident_f[:]),
                              rk + ["ident_f"], [pkey])
                    if half == 0:
                        CP(stg[:, 0:512], pst[:], [pkey], [skey])
                    else:
                        ACT(stg[:, 512:1024], pst[:], AF.Copy, [pkey], [skey])
                if final:
                    dst = out_d[s, (t - 2) * 128:(t - 1) * 128, :]
                elif t < 2:
                    dst = co_d[s, t * 128:(t + 1) * 128, :]
                else:
                    dst = xo_d[s, (t - 2) * 128:(t - 1) * 128, :]
                DMA("sp", dst, stg[:], [skey], [("out", s, t)], ("stage", skey[1]))
                out_keys.append(("out", s, t))

        out_keys = []

        def norm_block(tok0, ntok, a_ap, b_ap, dst_fn, dst_keyf, akeys):
            tl = list(tiles_of(tok0, ntok))
            pst, pkey = ps[7], ("ps", 7)
            for dc in range(8):
                sqb, sqk = tmpb_rot.next()
                ACT(sqb[:, 0:ntok], R[:, dc, tok0:tok0 + ntok], AF.Square, [("R", dc, t) for t in tl], [sqk])
                MM(pst[:, 0:ntok], ones_b[:], sqb[:, 0:ntok], dc == 0, dc == 7, [sqk, "ones_b"], [pkey])
            ACT(rstd[:, 0:ntok], pst[:, 0:ntok], AF.Sqrt, [pkey], ["rstd"], scale=1.0 / D, bias=EPS)
            RECIP(rstd[:, 0:ntok], rstd[:, 0:ntok], ["rstd"], ["rstd"])
            for dc in range(8):
                tf, tk = tmpf_rot.next()
                STT(tf[:, 0:ntok], R[:, dc, tok0:tok0 + ntok], a_ap[:, dc:dc + 1], rstd[:, 0:ntok], ALU.mult, ALU.mult,
                    [("R", dc, t) for t in tl] + ["rstd"] + akeys, [tk])
                wk = [dst_keyf(dc, t) for t in tl]
                if b_ap is not None:
                    ACT(dst_fn(dc), tf[:, 0:ntok], AF.Identity, [tk] + akeys, wk, bias=b_ap[:, dc:dc + 1], scale=1.0)
                else:
                    ACT(dst_fn(dc), tf[:, 0:ntok], AF.Copy, [tk], wk)

        def layer_norm_phase(li, l, s):
            for ci, cond in enumerate((s, nseq)):
                TS(tmpf[0][:, 0:8], mod[:, li, 8:16, cond], 1.0, None, ALU.add, None, [("mod", li)], [("tmpf", 0)])
                TT(Acol[:, :, ci], tmpf[0][:, 0:8], ngT[:, l, :], ALU.mult, [("tmpf", 0), "ngT"], ["Acol"])
            norm_block(CBLK[0], CBLK[1], Acol[:, :, 1], mod[:, li, 0:8, nseq], lambda dc: hT[:, dc, 0:NCTX],
                       lambda dc, t: ("h", t), ["Acol", ("mod", li)])
            for (t0, n) in XBLK:
                norm_block(t0, n, Acol[:, :, 0], mod[:, li, 0:8, s], lambda dc, t0=t0, n=n: hT[:, dc, t0:t0 + n],
                           lambda dc, t: ("h", t), ["Acol", ("mod", li)])

        def linear_T(wsb, wkey, col0, ncols, tok0, ntok, pst, pkey, rhs_buf=None, rhs_keyf=None, kcs=8):
            rb = hT if rhs_buf is None else rhs_buf
            kf = (lambda kc, t: ("h", t)) if rhs_keyf is None else rhs_keyf
            tl = list(tiles_of(tok0, ntok))
            for kc in range(kcs):
                MM(pst[0:ncols, 0:ntok], wsb[:, kc, col0:col0 + ncols], rb[:, kc, tok0:tok0 + ntok],
                   kc == 0, kc == kcs - 1, [wkey] + [kf(kc, t) for t in tl], [pkey])

        def residual_update(li, s, wsb, wkey, dcl, src_buf, src_keyf, need_ctx, psrot):
            blks = ([CBLK] if need_ctx else []) + XBLK
            for cc, dc in enumerate(dcl):
                for (t0, n) in blks:
                    pst, pkey = psrot.next()
                    tl = list(tiles_of(t0, n))
                    for kc in range(8):
                        MM(pst[:, 0:n], wsb[:, kc, cc * 128:(cc + 1) * 128], src_buf[:, kc, t0:t0 + n],
                           kc == 0, kc == 7, [wkey] + [src_keyf(kc, t) for t in tl], [pkey])
                    cond = nseq if t0 == 0 else s
                    rk = [("R", dc, t) for t in tl]
                    STT(R[:, dc, t0:t0 + n], pst[:, 0:n], mod[:, li, 16 + dc, cond:cond + 1], R[:, dc, t0:t0 + n],
                        ALU.mult, ALU.add, [pkey, ("mod", li)] + rk, rk)

        if has_a:
            spA = P.sbuf("spA_sb", [128, 128], BF16)
            mprev = P.sbuf("mprev_sb", [128, 128], BF16)
            mnext = P.sbuf("mnext_sb", [128, 128], BF16)
            esink = P.sbuf("esink_sb", [128, 2, 16], F32)
            DMA("pool", spA[:], spA_d, [], ["spA"], "c_spA")
            DMA("pool", mprev[:], mprev_d, [], ["mprev"], "c_mp")
            DMA("pool", mnext[:], mnext_d, [], ["mnext"], "c_mn")
            DMA("sp", esink[:], a_sink_d.rearrange("j p h -> p j h"), [], ["esink"], "c_sink")
            ACT(esink[:], esink[:], AF.Exp, ["esink"], ["esink"])

        def gqa_layer(li, l, s):
            j = l // 3
            need_ctx = l < DEPTH - 1
            QO = carve_bf(0, 8 * NT).rearrange("p (c t) -> p c t", t=NT)
            KT = carve_bf(36 * 1024, 2 * NT).rearrange("p (c t) -> p c t", t=NT)
            Vb = carve_bf(45 * 1024, NTILE * 384).rearrange("p (t c) -> p t c", c=384)
            VOFF = (0, 64, 192, 256)
            ropeA_cos = carve_bf(45 * 1024, NX)
            ropeA_sin = carve_bf(45 * 1024 + 4096, NX)
            DMA("pool", ropeA_cos, ropeA_cos_d, [], ["ropeA"], "c_rc")
            DMA("pool", ropeA_sin, ropeA_sin_d, [], ["ropeA"], "c_rs")
            layer_norm_phase(li, l, s)
            psrot = Rot([(ps[i], ("ps", i)) for i in (2, 3, 4, 5)])
            ps2rot = Rot([(ps[i], ("ps", i)) for i in (6, 7)])

            def rope_evac(pst, pkey, t0, n, dst, dkeys):
                if t0 < NCTX:
                    ACT(dst, pst[:, 0:n], AF.Copy, [pkey], dkeys)
                    return
                x0 = t0 - NCTX
                qraw, qk = tmpb_rot.next()
                ACT(qraw[:, 0:n], pst[:, 0:n], AF.Copy, [pkey], [qk])
                p2, p2k = ps2rot.next()
                MM(p2[:, 0:n], spA[:], qraw[:, 0:n], True, True, ["spA", qk], [p2k])
                t1, t1k = tmpf_rot.next()
                t2, t2k = tmpf_rot.next()
                TT(t1[:, 0:n], qraw[:, 0:n], ropeA_cos[:, x0:x0 + n], ALU.mult, [qk, "ropeA"], [t1k])
                TT(t2[:, 0:n], p2[:, 0:n], ropeA_sin[:, x0:x0 + n], ALU.mult, [p2k, "ropeA"], [t2k])
                TT(dst, t1[:, 0:n], t2[:, 0:n], ALU.add, [t1k, t2k], dkeys)

            for cg in range(2):
                wsb, wkey = ws_rot.next()
                load_w(wsb[:], wkey, a_win_d[j, :, cg * 512:(cg + 1) * 512].rearrange("(kc p) c -> p kc c", p=128),
                       sem=("ws", wkey[1]))
                for cc in range(4):
                    cq = cg * 4 + cc
                    for (t0, n) in ([CBLK] if need_ctx else []) + XBLK:
                        pst, pkey = psrot.next()
                        linear_T(wsb, wkey, cc * 128, 128, t0, n, pst, pkey)
                        rope_evac(pst, pkey, t0, n, QO[:, cq, t0:t0 + n], [("q", cq, t) for t in tiles_of(t0, n)])
            wsb, wkey = ws_rot.next()
            load_w(wsb[:], wkey, a_win_d[j, :, 1024:1536].rearrange("(kc p) c -> p kc c", p=128), sem=("ws", wkey[1]))
            for cc in range(2):
                for (t0, n) in [CBLK] + XBLK:
                    pst, pkey = psrot.next()
                    linear_T(wsb, wkey, cc * 128, 128, t0, n, pst, pkey)
                    rope_evac(pst, pkey, t0, n, KT[:, cc, t0:t0 + n], [("k", cc, t) for t in tiles_of(t0, n)])
            P.add("dve", lambda e: e.memset(Vb[:, :, 64:128], 1.0), [], ["vones", "ropeA"])
            P.add("dve", lambda e: e.memset(Vb[:, :, 256:320], 1.0), [], ["vones", "ropeA"])
            for t in range(NTILE):
                pst, pkey = psrot.next()
                for kc in range(8):
                    MM(pst[:, 0:256], hT[:, kc, t * 128:(t + 1) * 128], wsb[:, kc, 256:512], kc == 0, kc == 7,
                       [wkey, ("h", t)], [pkey])
                CP(Vb[:, t, 0:64], pst[:, 0:64], [pkey, "vones"], [("v", t)])
                CP(Vb[:, t, 128:256], pst[:, 64:192], [pkey], [("v", t)])
                CP(Vb[:, t, 320:384], pst[:, 192:256], [pkey], [("v", t)])

            PT_OFF = 45 * 1024 + NTILE * 384 * 2
            ptrot = Rot([(carve_bf(PT_OFF + i * 1024, 512), ("pt", i)) for i in range(5)] + [(tmpb[0][:], ("tmpb", 0))])
            sc_rot = Rot([(ps[i], ("ps", i)) for i in (4, 5, 6, 7)])
            tiles = []
            jobno = [0]

            def add_qtile(T, key_tiles):
                for jj in range(2):
                    pair = []
                    for half in range(2):
                        ob = jobno[0] % 4
                        jobno[0] += 1
                        pair.append((half, ps[ob], ("ps", ob)))
                        for ki, (kt, msk) in enumerate(key_tiles):
                            tiles.append(dict(T=T, jj=jj, half=half, kt=kt, msk=msk, first=(ki == 0),
                                              last=(ki == len(key_tiles) - 1), OP=ps[ob], OPk=("ps", ob), pair=pair))

            for qt in range(16):
                kts = [(0, None), (1, None)]
                if qt > 0:
                    kts.append((2 + qt - 1, (mprev, "mprev")))
                kts.append((2 + qt, None))
                if qt < 15:
                    kts.append((2 + qt + 1, (mnext, "mnext")))
                add_qtile(2 + qt, kts)
            if need_ctx:
                for T in range(2):
                    add_qtile(T, [(0, None), (1, None)])

            def att_s1(tl_):
                T, jj, half, kt = tl_["T"], tl_["jj"], tl_["half"], tl_["kt"]
                base = 64 * half
                sp_, spk = sc_rot.next()
                MM(sp_[:].rearrange("p (a b) -> p a b", b=128), KT[base:base + 64, jj, kt * 128:(kt + 1) * 128],
                   QO[base:base + 64, jj * 4:jj * 4 + 4, T * 128:(T + 1) * 128], True, True,
                   [("q", jj * 4 + g, T) for g in range(4)] + [("k", jj, kt)], [spk])
                pt, ptk = ptrot.next()
                ACT(pt, sp_[:], AF.Exp, [spk], [ptk], scale=0.125)
                if tl_["msk"] is not None:
                    msk = tl_["msk"]
                    ptv = pt.rearrange("p (a b) -> p a b", b=128)
                    TT(ptv, ptv, msk[0][:].unsqueeze(1).to_broadcast([128, 4, 128]), ALU.mult, [ptk, msk[1]], [ptk])
                tl_["pt"] = (pt, ptk)

            def att_s2(tl_):
                T, jj, half, kt = tl_["T"], tl_["jj"], tl_["half"], tl_["kt"]
                kvh = 2 * jj + half
                pt, ptk = tl_["pt"]
                MM(tl_["OP"][:], Vb[:, kt, VOFF[kvh]:VOFF[kvh] + 128], pt, tl_["first"], tl_["last"],
                   [ptk, ("v", kt), "vones"], [tl_["OPk"]])
                if not (tl_["last"] and half == 1):
                    return
                for (hf, OP, OPk) in tl_["pair"]:
                    kv = 2 * jj + hf
                    ob, db = 64 * hf, 64 * (1 - hf)
                    den, denk = tmpf_rot.next()
                    CP(den[ob:ob + 64, :], OP[db:db + 64, :], [OPk], [denk])
                    for g in range(4):
                        h = kv * 4 + g
                        TS(den[ob:ob + 64, g * 128:(g + 1) * 128], den[ob:ob + 64, g * 128:(g + 1) * 128],
                           esink[ob:ob + 64, j, h:h + 1], None, ALU.add, None, [denk, "esink"], [denk])
                    ACT(den[ob:ob + 64, :], den[ob:ob + 64, :], AF.Ln, [denk], [denk])
                    ACT(den[ob:ob + 64, :], den[ob:ob + 64, :], AF.Exp, [denk], [denk], scale=-1.0)
                    TT(QO[ob:ob + 64, jj * 4:jj * 4 + 4, T * 128:(T + 1) * 128],
                       OP[ob:ob + 64, :].rearrange("p (a b) -> p a b", b=128),
                       den[ob:ob + 64, :].rearrange("p (a b) -> p a b", b=128), ALU.mult, [OPk, denk],
                       [("q", jj * 4 + g, T) for g in range(4)])

            G = 3
            groups = [tiles[i:i + G] for i in range(0, len(tiles), G)]
            emit_skewed([((lambda gr=gr: [att_s1(t_) for t_ in gr]), (lambda gr=gr: [att_s2(t_) for t_ in gr]))
                         for gr in groups], 1)

            blks = ([CBLK] if need_ctx else []) + XBLK
            for cg in range(2):
                wsb, wkey = ws_rot.next()
                load_w(wsb[:], wkey, a_win_d[j, :, 1536 + cg * 512:1536 + (cg + 1) * 512].rearrange("(kc p) c -> p kc c", p=128),
                       sem=("ws", wkey[1]))
                for cc in range(4):
                    c = cg * 4 + cc
                    for (t0, n) in blks:
                        pst, pkey = psrot.next()
                        linear_T(wsb, wkey, cc * 128, 128, t0, n, pst, pkey)
                        sg, sgk = tmpb_rot.next()
                        ACT(sg[:, 0:n], pst[:, 0:n], AF.Silu, [pkey], [sgk])
                        qk = [("q", c, t) for t in tiles_of(t0, n)]
                        TT(QO[:, c, t0:t0 + n], QO[:, c, t0:t0 + n], sg[:, 0:n], ALU.mult, [sgk] + qk, qk)
            for dg in range(2):
                wsb, wkey = ws_rot.next()
                load_w(wsb[:], wkey, a_wout_d[j, :, dg * 512:(dg + 1) * 512].rearrange("(kc p) c -> p kc c", p=128),
                       sem=("ws", wkey[1]))
                residual_update(li, s, wsb, wkey, [dg * 4 + q for q in range(4)], QO, lambda kc, t: ("q", kc, t),
                                need_ctx, psrot)


        if has_b:
            spB = P.sbuf("spB_sb", [128, 128], BF16)
            qnT = P.sbuf("qnT_sb", [128, 4], F32)
            kvnT = P.sbuf("kvnT_sb", [128, 2], F32)
            DMA("pool", spB[:], spB_d, [], ["spB"], "c_spB")
            DMA("sp", qnT[:], b_qn_d[0], [], ["qnT"], "c_qn")
            DMA("sp", kvnT[:], b_kvn_d[0], [], ["kvnT"], "c_kvn")

        def mla_layer(li, l, s):
            j = 0
            need_ctx = l < DEPTH - 1
            SC = 96.0 ** -0.5
            CQ = carve_bf(0, 4 * NT).rearrange("p (c t) -> p c t", t=NT)
            CKV = carve_bf(18 * 1024, 2 * NT).rearrange("p (c t) -> p c t", t=NT)
            Q2 = carve_bf(27 * 1024, 2 * NT).rearrange("p (h t) -> p h t", t=NT)
            K2 = carve_bf(27 * 1024 + 9216, 2 * NT).rearrange("p (h t) -> p h t", t=NT)
            OG = carve_bf(27 * 1024 + 4 * 4608, NT)
            Vp = carve_bf(27 * 1024 + 5 * 4608, NTILE * 192).rearrange("p (t c) -> p t c", c=192)
            P.add("dve", lambda e: e.memset(Vp[:, :, 64:128], 1.0), [], ["vones"])
            RB_OFF = 27 * 1024 + 5 * 4608 + NTILE * 192 * 2
            ropeB_cos = carve_bf(RB_OFF, NX)
            ropeB_sin = carve_bf(RB_OFF + 4096, NX)
            DMA("pool", ropeB_cos, ropeB_cos_d, [], ["ropeB"], "c_rc")
            DMA("pool", ropeB_sin, ropeB_sin_d, [], ["ropeB"], "c_rs")
            layer_norm_phase(li, l, s)
            psrot = Rot([(ps[i], ("ps", i)) for i in (2, 3, 4, 5)])
            ps2rot = Rot([(ps[i], ("ps", i)) for i in (6, 7)])
            qblks = ([CBLK] if need_ctx else []) + XBLK
            ablks = [CBLK] + XBLK

            def rope_hi(pst, pkey, t0, n, dsts_hi, dkeys, dst_lo=None, lokeys=None):
                if dst_lo is not None:
                    CP(dst_lo, pst[0:64, 0:n], [pkey], lokeys)
                if t0 < NCTX:
                    for d_ in dsts_hi:
                        ACT(d_, pst[64:128, 0:n], AF.Copy, [pkey], dkeys)
                    return
                x0 = t0 - NCTX
                qraw, qk = tmpb_rot.next()
                ACT(qraw[:, 0:n], pst[:, 0:n], AF.Copy, [pkey], [qk])
                p2, p2k = ps2rot.next()
                MM(p2[:, 0:n], spB[:], qraw[:, 0:n], True, True, ["spB", qk], [p2k])
                t1, t1k = tmpf_rot.next()
                t2, t2k = tmpf_rot.next()
                TT(t1[64:128, 0:n], qraw[64:128, 0:n], ropeB_cos[64:128, x0:x0 + n], ALU.mult, [qk, "ropeB"], [t1k])
                TT(t2[64:128, 0:n], p2[64:128, 0:n], ropeB_sin[64:128, x0:x0 + n], ALU.mult, [p2k, "ropeB"], [t2k])
                for d_ in dsts_hi:
                    TT(d_, t1[64:128, 0:n], t2[64:128, 0:n], ALU.add, [t1k, t2k], dkeys)

            def lora_norm(wsb, wkey, col0, nch, gcol, dst, dkey):
                for (t0, n) in ablks:
                    tl = list(tiles_of(t0, n))
                    banks = [(ps[c], ("ps", c)) for c in range(nch)]
                    for c in range(nch):
                        linear_T(wsb, wkey, col0 + c * 128, 128, t0, n, banks[c][0], banks[c][1])
                    pst, pkey = ps[7], ("ps", 7)
                    for c in range(nch):
                        sqb, sqk = tmpb_rot.next()
                        ACT(sqb[:, 0:n], banks[c][0][:, 0:n], AF.Square, [banks[c][1]], [sqk])
                        MM(pst[:, 0:n], ones_b[:], sqb[:, 0:n], c == 0, c == nch - 1, [sqk, "ones_b"], [pkey])
                    ACT(rstd[:, 0:n], pst[:, 0:n], AF.Sqrt, [pkey], ["rstd"], scale=1.0 / (nch * 128), bias=EPS)
                    RECIP(rstd[:, 0:n], rstd[:, 0:n], ["rstd"], ["rstd"])
                    for c in range(nch):
                        STT(dst[:, c, t0:t0 + n], banks[c][0][:, 0:n], gcol[:, c:c + 1], rstd[:, 0:n], ALU.mult, ALU.mult,
                            [banks[c][1], "rstd", "qnT", "kvnT"], [(dkey, c, t) for t in tl])

            wsb, wkey = ws_rot.next()
            load_w(wsb[:], wkey, b_win_d[j, :, 0:512].rearrange("(kc p) c -> p kc c", p=128), sem=("ws", wkey[1]))
            lora_norm(wsb, wkey, 0, 4, qnT, CQ, "cq")
            wsb, wkey = ws_rot.next()
            P.add("dve", lambda e, ap=wsb[:, :, 256:384]: e.memset(ap, 0.0), [], [wkey])
            load_w(wsb[:, :, 0:256], wkey, b_win_d[j, :, 512:768].rearrange("(kc p) c -> p kc c", p=128), sem=("ws", wkey[1]))
            load_w(wsb[:, :, 320:352], wkey, b_win_d[j, :, 768:800].rearrange("(kc p) c -> p kc c", p=128), sem=("ws", wkey[1]))
            lora_norm(wsb, wkey, 0, 2, kvnT, CKV, "ckv")
            for (t0, n) in ablks:
                pst, pkey = psrot.next()
                linear_T(wsb, wkey, 256, 128, t0, n, pst, pkey)
                rope_hi(pst, pkey, t0, n, [K2[64:128, 0, t0:t0 + n], K2[64:128, 1, t0:t0 + n]],
                        [("k2r", t) for t in tiles_of(t0, n)])

            for (wsb_, wkey_) in ws_rot.items:
                wq_ = wsb_[:].rearrange("p a b -> p (a b)")[:, 0:1024].rearrange("p (k c) -> p k c", c=256)
                for hh in range(2):
                    P.add("dve", lambda e, ap=wq_[:, :, hh * 128 + 96:hh * 128 + 128]: e.memset(ap, 0.0), [], [wkey_])
            for c in range(8):
                wsb, wkey = ws_rot.next()
                wflat = wsb[:].rearrange("p a b -> p (a b)")
                wq = wflat[:, 0:1024].rearrange("p (k c) -> p k c", c=256)
                wkv = wflat[:, 1024:1536].rearrange("p (k c) -> p k c", c=256)
                wg = wflat[:, 1536:2560].rearrange("p (k c) -> p k c", c=128)
                wo = wflat[:, 2560:3584]
                sem = ("ws", wkey[1])
                for hh in range(2):
                    h = 2 * c + hh
                    load_w(wq[:, :, hh * 128:hh * 128 + 96], wkey,
                           b_wuq_d[j, :, h * 96:(h + 1) * 96].rearrange("(kc p) c -> p kc c", p=128), sem=sem)
                load_w(wkv, wkey, b_wukv_d[j, :, c * 256:(c + 1) * 256].rearrange("(kc p) c -> p kc c", p=128), sem=sem)
                load_w(wg, wkey, b_win_d[j, :, 800 + c * 128:800 + (c + 1) * 128].rearrange("(kc p) c -> p kc c", p=128), sem=sem)
                load_w(wo, wkey, b_wout_d[j, c * 128:(c + 1) * 128, :], sem=sem)
                cqk = lambda kc, t: ("cq", kc, t)
                ckvk = lambda kc, t: ("ckv", kc, t)
                for (t0, n) in qblks:
                    tl = list(tiles_of(t0, n))
                    for hh in range(2):
                        pst, pkey = psrot.next()
                        linear_T(wq, wkey, hh * 128, 128, t0, n, pst, pkey, rhs_buf=CQ, rhs_keyf=cqk, kcs=4)
                        rope_hi(pst, pkey, t0, n, [Q2[64:128, hh, t0:t0 + n]], [("q2", hh, t) for t in tl],
                                dst_lo=Q2[0:64, hh, t0:t0 + n], lokeys=[("q2", hh, t) for t in tl])
                for (t0, n) in ablks:
                    tl = list(tiles_of(t0, n))
                    pst, pkey = psrot.next()
                    linear_T(wkv, wkey, 0, 128, t0, n, pst, pkey, rhs_buf=CKV, rhs_keyf=ckvk, kcs=2)
                    CP(K2[0:64, 0, t0:t0 + n], pst[0:64, 0:n], [pkey], [("kn", 0, t) for t in tl])
                    ACT(K2[0:64, 1, t0:t0 + n], pst[64:128, 0:n], AF.Copy, [pkey], [("kn", 1, t) for t in tl])
                for t in range(NTILE):
                    pst, pkey = psrot.next()
                    for kc in range(2):
                        MM(pst[:, 0:128], CKV[:, kc, t * 128:(t + 1) * 128], wkv[:, kc, 128:256], kc == 0, kc == 1,
                           [wkey, ("ckv", kc, t)], [pkey])
                    CP(Vp[:, t, 0:64], pst[:, 0:64], [pkey], [("v", t)])
                    CP(Vp[:, t, 128:192], pst[:, 64:128], [pkey], [("v", t)])
                jobs = []
                for (t0, n) in qblks:
                    tl = list(tiles_of(t0, n))
                    kts = [0, 1] if t0 < NCTX else list(range(NTILE))
                    for hh in range(2):
                        OP, OPk = ps[hh], ("ps", hh)
                        for ki, kt in enumerate(kts):
                            st = {}

                            def s1(kt=kt, t0=t0, n=n, tl=tl, hh=hh, st=st):
                                sp_, spk = psrot.next()
                                MM(sp_[:, 0:n], K2[:, hh, kt * 128:(kt + 1) * 128], Q2[:, hh, t0:t0 + n], True, True,
                                   [("kn", hh, kt), ("k2r", kt)] + [("q2", hh, t) for t in tl], [spk])
                                pt, ptk = tmpb_rot.next()
                                ACT(pt[:, 0:n], sp_[:, 0:n], AF.Exp, [spk], [ptk], scale=SC)
                                st["pt"] = (pt, ptk)

                            def s2(kt=kt, ki=ki, nk=len(kts), t0=t0, n=n, tl=tl, hh=hh, OP=OP, OPk=OPk, st=st):
                                pt, ptk = st["pt"]
                                voff = 64 * hh
                                MM(OP[:, 0:n], Vp[:, kt, voff:voff + 128], pt[:, 0:n], ki == 0, ki == nk - 1,
                                   [ptk, ("v", kt), "vones"], [OPk])
                                if ki == nk - 1:
                                    ob, db = 64 * hh, 64 * (1 - hh)
                                    den, denk = tmpf_rot.next()
                                    CP(den[ob:ob + 64, 0:n], OP[db:db + 64, 0:n], [OPk], [denk])
                                    RECIP(den[ob:ob + 64, 0:n], den[ob:ob + 64, 0:n], [denk], [denk])
                                    TT(OG[ob:ob + 64, t0:t0 + n], OP[ob:ob + 64, 0:n], den[ob:ob + 64, 0:n], ALU.mult,
                                       [OPk, denk], [("og", t) for t in tl])

                            jobs.append((s1, s2))
                emit_skewed(jobs, 2)
                gjobs = []
                for (t0, n) in qblks:
                    def g1(t0=t0, n=n):
                        tl = list(tiles_of(t0, n))
                        pst, pkey = psrot.next()
                        linear_T(wg, wkey, 0, 128, t0, n, pst, pkey)
                        sg, sgk = tmpb_rot.next()
                        ACT(sg[:, 0:n], pst[:, 0:n], AF.Silu, [pkey], [sgk])
                        ogk = [("og", t) for t in tl]
                        TT(OG[:, t0:t0 + n], OG[:, t0:t0 + n], sg[:, 0:n], ALU.mult, [sgk] + ogk, ogk)

                    def g2(t0=t0, n=n):
                        tl = list(tiles_of(t0, n))
                        ogk = [("og", t) for t in tl]
                        cond = nseq if t0 == 0 else s
                        for dc in range(8):
                            pst, pkey = psrot.next()
                            MM(pst[:, 0:n], wo[:, dc * 128:(dc + 1) * 128], OG[:, t0:t0 + n], True, True, [wkey] + ogk, [pkey])
                            rk = [("R", dc, t) for t in tl]
                            STT(R[:, dc, t0:t0 + n], pst[:, 0:n], mod[:, li, 16 + dc, cond:cond + 1], R[:, dc, t0:t0 + n],
                                ALU.mult, ALU.add, [pkey, ("mod", li)] + rk, rk)

                    gjobs.append((g1, g2))
                emit_skewed(gjobs, 1)

        if has_c:
            triU = P.sbuf("triU_sb", [128, 128], F32)
            triL = P.sbuf("triL_sb", [128, 128], F32)
            triSL = P.sbuf("triSL_sb", [128, 128], F32)
            triSU = P.sbuf("triSU_sb", [128, 128], F32)
            ones_f = P.sbuf("ones_f_sb", [128, 128], F32)
            cwT = P.sbuf("cwT_sb", [128, 24, 3], F32)
            cbT = P.sbuf("cbT_sb", [128, 24], F32)
            dtb = P.sbuf("dtb_sb", [128, 64], F32)
            Aneg = P.sbuf("Aneg_sb", [128, 64], F32)
            dsk = P.sbuf("dsk_sb", [128, 32], F32)
            nwT = P.sbuf("nwT_sb", [128, 16], F32)
            for (dst, src, nm) in ((triU, triU_d, "triU"), (triL, triL_d, "triL"), (triSL, triSL_d, "triSL"),
                                   (triSU, triSU_d, "triSU"), (ones_f, ones_d, "ones_f"), (cwT, c_cwT_d, "cwT"),
                                   (cbT, c_cbT_d, "cbT"), (dtb, c_dtb_d, "dtb"), (Aneg, c_alog_d, "Aneg"),
                                   (dsk, c_dsk_d, "dsk"), (nwT, c_nwT_d, "nwT")):
                DMA("sp", dst[:], src, [], [nm], "c_" + nm)
            ACT(Aneg[:], Aneg[:], AF.Exp, ["Aneg"], ["Aneg"])
            TS(Aneg[:], Aneg[:], -1.0, None, ALU.mult, None, ["Aneg"], ["Aneg"])

        def ssd_layer(li, l, s):
            j = 0
            need_ctx = l < DEPTH - 1
            HP, NF, NCH = 8, 512, 4
            o = [0]

            def take_bf(nelem):
                ap = carve_bf(o[0], nelem)
                o[0] += nelem * 2
                return ap

            def take_f(nelem):
                ap = carve_f(o[0], nelem)
                o[0] += nelem * 4
                return ap

            xsT = take_bf(NCH * NT).rearrange("p (c t) -> p c t", t=NT)
            BT = take_bf(NT)
            CT = take_bf(NT)
            YP = take_bf(NTILE * NF).rearrange("p (t c) -> p t c", c=NF)
            dtt = take_f(NTILE * 2 * HP).rearrange("p (t c) -> p t c", c=2 * HP)
            _S1 = take_f(NF)
            _Sb1 = take_bf(NF)
            Sst = [_S1, _S1]
            Sb = [_Sb1, _Sb1]
            xs_tok2 = [take_bf(NF) for _ in range(2)]
            B_tok2 = [take_bf(128) for _ in range(2)]
            Xdt = take_bf(NF)
            Xds = take_bf(NF)
            Mt = take_bf(HP * 128).rearrange("p (r l) -> p r l", l=128)
            CBm2 = [[take_bf(128) for _ in range(2)] for _ in range(2)]
            sm = take_f(8 * 2 * HP)
            yacc = take_f(NF)
            yT = take_bf(NCH * 512).rearrange("p (c t) -> p c t", t=512)
            assert 27 * 1024 + 2 * NT * 4 <= o[0]
            layer_norm_phase(li, l, s)
            psrot = Rot([(ps[i], ("ps", i)) for i in (1, 7)])
            ablks = [CBLK] + XBLK
            SEGS = [(0, NCTX), (NCTX, NT)]

            raws = [carve_f(27 * 1024 + i * NT * 4, NT) for i in range(2)]
            raw_i = [0]

            def conv_chunk(wsb, wkey, col0, ch, dst_fn, dkeyf):
                ri = raw_i[0] % 2
                raw_i[0] += 1
                raw = raws[ri]
                rkey = lambda t: ("raw", ri, t)
                ypk = [("YP", t) for t in range(9 * ri, 9 * ri + 9)]
                for (t0, n) in ablks:
                    pst, pkey = psrot.next()
                    linear_T(wsb, wkey, col0, 128, t0, n, pst, pkey)
                    ACT(raw[:, t0:t0 + n], pst[:, 0:n], AF.Copy, [pkey], [rkey(t) for t in tiles_of(t0, n)] + ypk)
                jobs = []
                for (t0, n) in ablks:
                    st = {}

                    def c1(t0=t0, n=n, st=st):
                        tl = list(tiles_of(t0, n))
                        acc, acck = tmpf_rot.next()
                        st["acc"] = (acc, acck)
                        ACT(acc[:, 0:n], raw[:, t0:t0 + n], AF.Identity, [rkey(t) for t in tl] + ["cwT", "cbT"] + ypk, [acck],
                            scale=cwT[:, ch, 1:2], bias=cbT[:, ch:ch + 1])

                    def c2(t0=t0, n=n, st=st):
                        tl = list(tiles_of(t0, n))
                        acc, acck = st["acc"]
                        seg0, seg1 = (0, NCTX) if t0 < NCTX else (NCTX, NT)
                        lo = max(t0, seg0 + 1)
                        rk = [rkey(t) for t in range(max(t0 - 1, 0) // 128, min(t0 + n + 1, NT - 1) // 128 + 1)]
                        STT(acc[:, lo - t0:n], raw[:, lo - 1:t0 + n - 1], cwT[:, ch, 0:1], acc[:, lo - t0:n], ALU.mult, ALU.add,
                            rk + [acck, "cwT"] + ypk, [acck])
                        hi = min(t0 + n, seg1 - 1)
                        STT(acc[:, 0:hi - t0], raw[:, t0 + 1:hi + 1], cwT[:, ch, 2:3], acc[:, 0:hi - t0], ALU.mult, ALU.add,
                            rk + [acck, "cwT"] + ypk, [acck])
                        ACT(dst_fn(t0, n), acc[:, 0:n], AF.Silu, [acck], [dkeyf(t) for t in tl])

                    jobs.append((c1, c2))
                emit_skewed(jobs, 1)

            for g in range(4):
                wsb, wkey = ws_rot.next()
                load_w(wsb[:], wkey, c_win_d[j, :, 2048 + g * 512:2048 + (g + 1) * 512].rearrange("(kc p) c -> p kc c", p=128),
                       sem=("ws", wkey[1]))
                for c in range(NCH):
                    conv_chunk(wsb, wkey, c * 128, g * 4 + c, lambda t0, n, c=c: xsT[:, c, t0:t0 + n], lambda t, c=c: ("xs", c, t))
                wsb, wkey = ws_rot.next()
                sem = ("ws", wkey[1])
                load_w(wsb[:, :, 0:128], wkey, c_win_d[j, :, 4096 + g * 128:4096 + (g + 1) * 128].rearrange("(kc p) c -> p kc c", p=128), sem=sem)
                load_w(wsb[:, :, 128:256], wkey, c_win_d[j, :, 4608 + g * 128:4608 + (g + 1) * 128].rearrange("(kc p) c -> p kc c", p=128), sem=sem)
                for d in range(2):
                    c0 = 5120 + d * 32 + g * 8
                    load_w(wsb[:, :, 256 + d * 8:264 + d * 8], wkey, c_win_d[j, :, c0:c0 + 8].rearrange("(kc p) c -> p kc c", p=128), sem=sem)
                conv_chunk(wsb, wkey, 0, 16 + g, lambda t0, n: BT[:, t0:t0 + n], lambda t: ("B", t))
                conv_chunk(wsb, wkey, 128, 20 + g, lambda t0, n: CT[:, t0:t0 + n], lambda t: ("C", t))
                for t in range(NTILE):
                    pst, pkey = psrot.next()
                    for kc in range(8):
                        MM(pst[:, 0:16], hT[:, kc, t * 128:(t + 1) * 128], wsb[:, kc, 256:272], kc == 0, kc == 7,
                           [wkey, ("h", t)], [pkey])
                    for d in range(2):
                        TT(dtt[:, t, d * 8:(d + 1) * 8], pst[:, d * 8:(d + 1) * 8], dtb[:, d * 32 + g * 8:d * 32 + g * 8 + 8],
                           ALU.add, [pkey, "dtb"], [("dt", t)])
                    ACT(dtt[:, t, :], dtt[:, t, :], AF.Exp, [("dt", t)], [("dt", t)])
                    ACT(dtt[:, t, :], dtt[:, t, :], AF.Ln, [("dt", t)], [("dt", t)], bias=1.0, scale=1.0)
                wz, wzk = ws_rot.next()
                load_w(wz[:], wzk, c_win_d[j, :, g * 512:(g + 1) * 512].rearrange("(kc p) c -> p kc c", p=128), sem=("ws", wzk[1]))
                wo_, wok = ws_rot.next()
                wo = wo_[:].rearrange("p a b -> p (a b)").rearrange("p (k c) -> p k c", c=1024)
                load_w(wo, wok, c_wout_d[j, g * 512:(g + 1) * 512, :].rearrange("(kc p) c -> p kc c", p=128), sem=("ws", wok[1]))

                def scal(p, k):
                    return sm[:, p * 64 + k * 8:p * 64 + k * 8 + HP]

                def S1(T, p):
                    ptb = ps[7][:].bitcast(BF16)
                    for c in range(NCH):
                        P.add("pe", lambda e, o_=ptb[:, c * 128:(c + 1) * 128], i_=xsT[:, c, T * 128:(T + 1) * 128]:
                              e.transpose(o_, i_, ident_b[:]), [("xs", c, T), "ident_b"], [("ps", 7)])
                    P.add("pe", lambda e, o_=ptb[:, 512:640], i_=BT[:, T * 128:(T + 1) * 128]: e.transpose(o_, i_, ident_b[:]),
                          [("B", T), "ident_b"], [("ps", 7)])
                    ACT(xs_tok2[p][:], ptb[:, 0:512], AF.Copy, [("ps", 7)], [("xs_tok", p)])
                    ACT(B_tok2[p][:], ptb[:, 512:640], AF.Copy, [("ps", 7)], [("B_tok", p)])
                    MM(ps[1][:, 0:128], BT[:, T * 128:(T + 1) * 128], CT[:, T * 128:(T + 1) * 128], True, True,
                       [("B", T), ("C", T)], [("ps", 1)])
                    TT(CBm2[p][0][:], ps[1][:, 0:128], triU[:], ALU.mult, [("ps", 1), "triU"], [("CBm", p, 0)])
                    TT(CBm2[p][1][:], ps[1][:, 0:128], triL[:], ALU.mult, [("ps", 1), "triL"], [("CBm", p, 1)])

                def S2(T, d, p):
                    cum = triU if d == 0 else triL
                    ck = "triU" if d == 0 else "triL"
                    a_t, e_t, ds_t, cd_t, w2_t = (scal(p, k) for k in range(5))
                    dts = dtt[:, T, d * 8:(d + 1) * 8]
                    TT(a_t, dts, Aneg[:, d * 32 + g * 8:d * 32 + g * 8 + 8], ALU.mult, [("dt", T), "Aneg"], [("a_t", p)])
                    MM(ps[0][:, 0:HP], cum[:], a_t, True, True, [ck, ("a_t", p)], [("ps", 0)])
                    MM(ps[0][:, 64:64 + HP], ones_f[:], a_t, True, True, ["ones_f", ("a_t", p)], [("ps", 0)])
                    ACT(e_t, ps[0][:, 0:HP], AF.Exp, [("ps", 0)], [("e_t", p)])
                    ACT(cd_t, ps[0][:, 64:64 + HP], AF.Exp, [("ps", 0)], [("cd_t", p)])
                    CP(w2_t, ps[0][:, 0:HP], [("ps", 0)], [("w2_t", p)])
                    TT(ds_t, ps[0][:, 64:64 + HP], w2_t, ALU.subtract, [("ps", 0), ("w2_t", p)], [("ds_t", p)])
                    ACT(ds_t, ds_t, AF.Exp, [("ds_t", p)], [("ds_t", p)])

                def S3a(T, d, p):
                    cum, strict = (triU, triSL) if d == 0 else (triL, triSU)
                    ck, sk = ("triU", "triSL") if d == 0 else ("triL", "triSU")
                    a_t = scal(p, 0)
                    dts = dtt[:, T, d * 8:(d + 1) * 8]
                    xv = xs_tok2[p][:].rearrange("p (r q) -> p r q", q=64)
                    TT(Xdt[:].rearrange("p (r q) -> p r q", q=64), xv, dts.unsqueeze(2).to_broadcast([128, HP, 64]), ALU.mult,
                       [("xs_tok", p), ("dt", T)], ["Xdt"])
                    aus = []
                    for hq in range(HP // 4):
                        aU, aUk = tmpf_rot.next()
                        TT(aU[:].rearrange("p (r l) -> p r l", l=128), cum[:].unsqueeze(1).to_broadcast([128, 4, 128]),
                           a_t[:, hq * 4:(hq + 1) * 4].unsqueeze(2).to_broadcast([128, 4, 128]), ALU.mult, [ck, ("a_t", p)], [aUk])
                        aus.append((aU, aUk))
                    for hq in range(HP // 4):
                        MM(ps[2 + hq][:], strict[:], aus[hq][0][:], True, True, [sk, aus[hq][1]], [("ps", 2 + hq)])

                def S3b(T, d, p):
                    ds_t = scal(p, 2)
                    lms = []
                    for hq in range(HP // 4):
                        lm, lmk = tmpb_rot.next()
                        ACT(lm[:], ps[2 + hq][:], AF.Exp, [("ps", 2 + hq)], [lmk])
                        lms.append((lm, lmk))
                    for hq in range(HP // 4):
                        TT(Mt[:, hq * 4:(hq + 1) * 4, :], lms[hq][0][:].rearrange("p (r l) -> p r l", l=128),
                           CBm2[p][d][:].unsqueeze(1).to_broadcast([128, 4, 128]), ALU.mult, [lms[hq][1], ("CBm", p, d)], ["Mt"])
                    TT(Xds[:].rearrange("p (r q) -> p r q", q=64), Xdt[:].rearrange("p (r q) -> p r q", q=64),
                       ds_t.unsqueeze(2).to_broadcast([128, HP, 64]), ALU.mult, ["Xdt", ("ds_t", p)], ["Xds"])

                def S4(T, d, p):
                    for r in range(HP):
                        MM(ps[4][:, r * 64:(r + 1) * 64], Mt[:, r, :], Xdt[:, r * 64:(r + 1) * 64], r == 0, r == HP - 1,
                           ["Mt", "Xdt"], [("ps", 4)])
                    MM(ps[5][:], CT[:, T * 128:(T + 1) * 128], Sb[d][:], True, True, [("C", T), ("Sb", d)], [("ps", 5)])
                    MM(ps[6][:], B_tok2[p][:], Xds[:], True, True, [("B_tok", p), "Xds"], [("ps", 6)])

                def S5(T, d, p):
                    a_t, e_t, ds_t, cd_t, w2_t = (scal(p, k) for k in range(5))
                    yo, yok = tmpf_rot.next()
                    TT(yo[:].rearrange("p (r q) -> p r q", q=64), ps[5][:].rearrange("p (r q) -> p r q", q=64),
                       e_t.unsqueeze(2).to_broadcast([128, HP, 64]), ALU.mult, [("ps", 5), ("e_t", p)], [yok])
                    TT(yacc[:], ps[4][:], yo[:], ALU.add, [("ps", 4), yok], ["yacc"])
                    sv = Sst[d][:].rearrange("p (r q) -> p r q", q=64)
                    TT(sv, sv, cd_t.unsqueeze(2).to_broadcast([128, HP, 64]), ALU.mult, [("S", d), ("cd_t", p)], [("S", d)])
                    TT(Sst[d][:], Sst[d][:], ps[6][:], ALU.add, [("S", d), ("ps", 6)], [("S", d)])
                    ACT(Sb[d][:], Sst[d][:], AF.Copy, [("S", d)], [("Sb", d)])

                def run_pass(order, d, tail):
                    n_ = len(order)
                    P.add("dve", lambda e, ap=Sst[d][:]: e.memset(ap, 0.0), [("S", 1 - d), ("Sb", 1 - d)], [("S", d)])
                    P.add("dve", lambda e, ap=Sb[d][:]: e.memset(ap, 0.0), [("S", 1 - d), ("Sb", 1 - d)], [("Sb", d)])

                    def A(i):
                        S1(order[i], i % 2)
                        S2(order[i], d, i % 2)
                        S3a(order[i], d, i % 2)

                    A(0)
                    for i in range(n_):
                        if i > 0:
                            S5(order[i - 1], d, (i - 1) % 2)
                            tail(i - 1, order[i - 1], (i - 1) % 2)
                        S3b(order[i], d, i % 2)
                        S4(order[i], d, i % 2)
                        if i + 1 < n_:
                            A(i + 1)
                    S5(order[n_ - 1], d, (n_ - 1) % 2)
                    tail(n_ - 1, order[n_ - 1], (n_ - 1) % 2)


                def tail_f(i, T, p):
                    dx, dxk = tmpf_rot.next()
                    TT(dx[:].rearrange("p (r q) -> p r q", q=64), xs_tok2[p][:].rearrange("p (r q) -> p r q", q=64),
                       dsk[:, g * 8:(g + 1) * 8].unsqueeze(2).to_broadcast([128, HP, 64]), ALU.mult, [("xs_tok", p), "dsk"], [dxk])
                    TT(YP[:, T, :], yacc[:], dx[:], ALU.add, ["yacc", dxk], [("YP", T)])

                run_pass(list(range(NTILE)), 0, tail_f)

                border = [1, 0] + list(range(NTILE - 1, 1, -1))

                bblk = [[1, 0]] + [list(range(17 - 4 * q, 13 - 4 * q, -1)) for q in range(4)]
                blk_of = {}
                for bl in bblk:
                    for T_ in bl:
                        blk_of[T_] = bl

                def tail_b(i, T, p):
                    bl = blk_of[T]
                    t_lo = min(bl)
                    pz, pzk = ps[1], ("ps", 1)
                    for kc in range(8):
                        MM(pz[:], hT[:, kc, T * 128:(T + 1) * 128], wz[:, kc, :], kc == 0, kc == 7, [wzk, ("h", T)], [pzk])
                    zs, zsk = tmpf_rot.next()
                    ACT(zs[:], pz[:], AF.Silu, [pzk], [zsk])
                    TT(yacc[:], yacc[:], YP[:, T, :], ALU.add, ["yacc", ("YP", T)], ["yacc"])
                    TT(yacc[:], yacc[:], zs[:], ALU.mult, ["yacc", zsk], ["yacc"])
                    ss_t = sm[:, p * 64 + 40:p * 64 + 41]
                    rs_t = sm[:, p * 64 + 41:p * 64 + 42]
                    junk, jk = tmpf_rot.next()
                    P.add("dve", lambda e, o_=junk[:], i_=yacc[:], a_=ss_t: e.scalar_tensor_tensor(
                        out=o_, in0=i_, scalar=1.0, in1=i_, op0=ALU.mult, op1=ALU.mult, accum_out=a_), ["yacc"], [jk, ("ss_t", p)])
                    ACT(rs_t, ss_t, AF.Ln, [("ss_t", p)], [("rs_t", p)], scale=1.0 / NF, bias=EPS)
                    ACT(rs_t, rs_t, AF.Exp, [("rs_t", p)], [("rs_t", p)], scale=-0.5)
                    ynb, ynk = tmpb_rot.next()
                    TS(ynb[:], yacc[:], rs_t, None, ALU.mult, None, ["yacc", ("rs_t", p)], [ynk])
                    ptb = ps[7][:].bitcast(BF16)
                    for c in range(NCH):
                        P.add("pe", lambda e, o_=ptb[:, c * 128:(c + 1) * 128], i_=ynb[:, c * 128:(c + 1) * 128]:
                              e.transpose(o_, i_, ident_b[:]), [ynk, "ident_b"], [("ps", 7)])
                    off = (T - t_lo) * 128
                    for c in range(NCH):
                        TS(yT[:, c, off:off + 128], ptb[:, c * 128:(c + 1) * 128], nwT[:, g * 4 + c:g * 4 + c + 1], None,
                           ALU.mult, None, [("ps", 7), "nwT"], [("yT", T)])
                    if T != bl[-1]:
                        return
                    if t_lo == 0 and not need_ctx:
                        return
                    n = 128 * len(bl)
                    cond = nseq if t_lo == 0 else s
                    tl = list(range(t_lo, t_lo + len(bl)))
                    for dc in range(8):
                        po, pok = ps[4 + dc % 2], ("ps", 4 + dc % 2)
                        for kc in range(NCH):
                            MM(po[:, 0:n], wo[:, kc, dc * 128:(dc + 1) * 128], yT[:, kc, 0:n], kc == 0, kc == NCH - 1,
                               [wok] + [("yT", t) for t in tl], [pok])
                        rk = [("R", dc, t) for t in tl]
                        STT(R[:, dc, t_lo * 128:t_lo * 128 + n], po[:, 0:n], mod[:, li, 16 + dc, cond:cond + 1],
                            R[:, dc, t_lo * 128:t_lo * 128 + n], ALU.mult, ALU.add, [pok, ("mod", li)] + rk, rk)

                run_pass(border, 1, tail_b)

        ARENA_BYTES[0] = (int(nc.sbuf_bytes_remaining) // 64) * 64
        arena_box.append(P.sbuf("arena", [128, ARENA_BYTES[0] // 4], F32))
        stage = [carve_f(STAGE_OFF + i * 4096, D) for i in range(2)]
        stage_rot = Rot([(stage[i], ("stage", i)) for i in range(2)])
        for s in range(nseq):
            load_seq(s)
            for li, l in enumerate(layers):
                if l % 3 == 0:
                    gqa_layer(li, l, s)
                elif l % 3 == 1:
                    mla_layer(li, l, s)
                else:
                    ssd_layer(li, l, s)
                P.barrier()
            if final:
                fin = carve_f(0, 8 * 512).rearrange("p (c t) -> p c t", t=512)
                for (t0, n) in XBLK:
                    norm_block(t0, n, fgT, None, lambda dc: fin[:, dc, :], lambda dc, t: ("fin", dc, t), ["fgT"])
                    store_seq(s, fin, tiles=list(tiles_of(t0, n)), fin_t0=t0 // 128)
            else:
                store_seq(s, None)
            P.barrier()
        P.add("sp", lambda e: e.nop(), out_keys, [])
        P.emit()
        prog_stats = dict(n_ops=len(P.allops), n_waits=P.n_waits)
    return nc, prog_stats


_CONSTS = {}


def _consts():
    if _CONSTS:
        return _CONSTS
    c = _CONSTS
    c["ident"] = np.eye(128, dtype=np.float32)
    c["ones"] = np.ones((128, 128), np.float32)
    cosA, sinA, spA = rope_tables(64, 2)
    c["ropeA_cos"], c["ropeA_sin"], c["spA"] = cosA, sinA, spA
    cosB, sinB, spB = rope_tables(32, 4)
    c["ropeB_cos"], c["ropeB_sin"], c["spB"] = cosB, sinB, spB
    kj = np.arange(128)[:, None]
    qi = np.arange(128)[None, :]
    c["triU"] = np.ascontiguousarray((kj <= qi).astype(np.float32))
    c["triL"] = np.ascontiguousarray((kj >= qi).astype(np.float32))
    c["triSL"] = np.ascontiguousarray((kj > qi).astype(np.float32))
    c["triSU"] = np.ascontiguousarray((kj < qi).astype(np.float32))
    c["mask_prev"] = np.ascontiguousarray((kj >= qi).astype(np.float32))
    c["mask_next"] = np.ascontiguousarray((kj <= qi).astype(np.float32))
    return c


def _layout_params(inp):
    f = lambda a: np.ascontiguousarray(np.asarray(a, dtype=np.float32))
    p = {}
    p["ada_w"] = f(inp["ada_w"])
    p["ada_bT"] = f(np.asarray(inp["ada_b"]).reshape(DEPTH, 24, 128).transpose(0, 2, 1))
    p["norm_gT"] = f(np.asarray(inp["norm_g"]).reshape(DEPTH, 8, 128).transpose(0, 2, 1))
    p["final_gT"] = f(np.asarray(inp["final_g"]).reshape(8, 128).T)
    qperm, operm = gqa_perms()
    awin = np.asarray(inp["a_w_in"])
    awp = awin.copy()
    awp[:, :, 0:1024] = awin[:, :, qperm]
    awp[:, :, 1536:2560] = awin[:, :, 1536 + qperm]
    p["a_w_in"] = f(awp)
    p["a_w_out"] = f(np.asarray(inp["a_w_out"])[:, qperm, :])
    p["a_sink_bc"] = f(np.broadcast_to(np.asarray(inp["a_sink"])[:, None, :], (2, 128, 16)))
    uq = np.zeros(1536, np.int64)
    ukv = np.zeros(2048, np.int64)
    for c in range(8):
        A, Bh = 2 * c, 2 * c + 1
        uq[c * 192:(c + 1) * 192] = np.concatenate([A * 96 + np.arange(64), Bh * 96 + np.arange(64),
                                                    A * 96 + 64 + np.arange(32), Bh * 96 + 64 + np.arange(32)])
        ukv[c * 256:(c + 1) * 256] = np.concatenate([A * 128 + np.arange(64), Bh * 128 + np.arange(64),
                                                     A * 128 + 64 + np.arange(64), Bh * 128 + 64 + np.arange(64)])
    p["c_w_in"] = f(inp["c_w_in"])
    p["c_w_out"] = f(inp["c_w_out"])
    p["c_conv_wT"] = f(np.asarray(inp["c_conv_w"])[0].reshape(3, 24, 128).transpose(2, 1, 0))
    p["c_conv_bT"] = f(np.asarray(inp["c_conv_b"])[0].reshape(24, 128).T)
    p["c_dt_bias_bc"] = f(np.broadcast_to(np.asarray(inp["c_dt_bias"])[0].reshape(1, 64), (128, 64)))
    p["c_a_log_bc"] = f(np.broadcast_to(np.asarray(inp["c_a_log"])[0].reshape(1, 64), (128, 64)))
    p["c_d_bc"] = f(np.broadcast_to(np.asarray(inp["c_d"])[0].reshape(1, 32), (128, 32)))
    p["c_normT"] = f(np.asarray(inp["c_norm"])[0].reshape(16, 128).T)
    p["b_w_in"] = f(inp["b_w_in"])
    p["b_w_uq"] = f(inp["b_w_uq"])
    p["b_w_ukv"] = f(np.asarray(inp["b_w_ukv"])[:, :, ukv])
    p["b_w_out"] = f(inp["b_w_out"])
    p["b_q_normT"] = f(np.asarray(inp["b_q_norm"]).reshape(1, 4, 128).transpose(0, 2, 1))
    p["b_kv_normT"] = f(np.asarray(inp["b_kv_norm"]).reshape(1, 2, 128).transpose(0, 2, 1))
    return p


def _cond_T(c_rows, c_ctx):
    allc = np.concatenate([np.asarray(c_rows), np.asarray(c_ctx)[None, :]], axis=0)
    return np.ascontiguousarray(allc.reshape(-1, 8, 128).transpose(2, 1, 0).astype(np.float32))


_A_NAMES = ("a_w_in", "a_sink_bc", "a_w_out", "ropeA_cos", "ropeA_sin", "spA", "mask_prev", "mask_next")
_C_NAMES = ("c_w_in", "c_w_out", "c_conv_wT", "c_conv_bT", "c_dt_bias_bc", "c_a_log_bc", "c_d_bc", "c_normT",
            "triU", "triL", "triSL", "triSU")
_B_NAMES = ("b_w_in", "b_w_uq", "b_w_ukv", "b_w_out", "b_q_normT", "b_kv_normT", "ropeB_cos", "ropeB_sin", "spB")


def run_layers(layers, final, x, ctx, c, c_ctx, params, n_cores=N_CORES, core_ids=None):
    B = x.shape[0]
    nseq = B // n_cores
    nc, stats = build_program(layers, nseq, final)
    consts = _consts()
    in_maps = []
    for k in range(n_cores):
        sl = slice(k * nseq, (k + 1) * nseq)
        m = {"x": np.ascontiguousarray(x[sl]), "ctx": np.ascontiguousarray(ctx[sl]),
             "condT": _cond_T(c[sl], c_ctx), "ada_w": params["ada_w"], "ada_bT": params["ada_bT"],
             "norm_gT": params["norm_gT"], "final_gT": params["final_gT"],
             "ident": consts["ident"], "ones": consts["ones"]}
        if any(l % 3 == 0 for l in layers):
            for nme in _A_NAMES:
                m[nme] = params[nme] if nme in params else consts[nme]
        if any(l % 3 == 1 for l in layers):
            for nme in _B_NAMES:
                m[nme] = params[nme] if nme in params else consts[nme]
        if any(l % 3 == 2 for l in layers):
            for nme in _C_NAMES:
                m[nme] = params[nme] if nme in params else consts[nme]
        in_maps.append(m)
    res = run_bass_kernel_spmd(nc, in_maps, core_ids=list(range(n_cores)) if core_ids is None else core_ids)
    if final:
        return np.concatenate([r["out"] for r in res.results], axis=0), None
    return (np.concatenate([r["x_out"] for r in res.results], axis=0),
            np.concatenate([r["ctx_out"] for r in res.results], axis=0))


LAUNCH_PLAN = [[0, 1, 2, 3]]


def kernel(**inputs):
    inp = {k: np.asarray(v) for k, v in inputs.items()}
    params = _layout_params(inp)
    x = np.ascontiguousarray(inp["x"], dtype=np.float32)
    ctx = np.ascontiguousarray(inp["ctx"], dtype=np.float32)
    c = np.asarray(inp["c"], dtype=np.float32)
    c_ctx = np.asarray(inp["c_ctx"], dtype=np.float32)
    out = None
    for gi, layers in enumerate(LAUNCH_PLAN):
        final = gi == len(LAUNCH_PLAN) - 1
        a, b = run_layers(layers, final, x, ctx, c, c_ctx, params)
        if final:
            out = a
        else:
            x, ctx = a, b
    return out.astype(np.float32)
```

```python
from contextlib import ExitStack
import math
import numpy as np
import concourse.bass as bass
import concourse.mybir as mybir
from concourse.bass_utils import run_bass_kernel_spmd

F32 = mybir.dt.float32
BF16 = mybir.dt.bfloat16
AF = mybir.ActivationFunctionType
ALU = mybir.AluOpType
AX = mybir.AxisListType

ENGS = ("pe", "act", "dve", "pool", "sp")
N_CORES = 8
D = 1024
NX = 2048
NCTX = 256
NT = NX + NCTX
NTILE = NT // 128
EPS = 1e-6
DEPTH = 4


class Op:
    __slots__ = ("eng", "fn", "deps", "seq", "inc", "count", "chan", "waits", "known", "is_dma")


class Prog:
    def __init__(self, nc, stack):
        self.nc = nc
        self.stack = stack
        self.ops = {e: [] for e in ENGS}
        self.allops = []
        self.last_w = {}
        self.readers = {}
        self.chan_ops = {}
        self.sems = {}

    def sbuf(self, name, shape, dt):
        return self.stack.enter_context(self.nc.sbuf_tensor(name, list(shape), dt))

    def psum(self, name, shape, dt):
        return self.stack.enter_context(self.nc.psum_tensor(name, list(shape), dt))

    def add(self, eng, fn, r=(), w=(), dma=None, extra_deps=None):
        o = Op()
        o.eng = eng
        o.fn = fn
        o.is_dma = dma is not None
        o.chan = ("dma", dma) if o.is_dma else eng
        deps = {}
        for k in r:
            x = self.last_w.get(k)
            if x is not None:
                deps[id(x)] = (x, True)
            if isinstance(k, tuple) and k[0] == "ps":
                for x in self.readers.get(k, ()):
                    if x.eng != eng and id(x) not in deps:
                        deps[id(x)] = (x, False)
        for k in w:
            x = self.last_w.get(k)
            if x is not None and id(x) not in deps:
                deps[id(x)] = (x, "waw")
            for x in self.readers.get(k, ()):
                if id(x) not in deps or deps[id(x)][1] == "waw":
                    deps[id(x)] = (x, False)
        dl = []
        for x, raw in deps.values():
            if (not x.is_dma) and (not o.is_dma) and x.eng == eng and raw is not True and (eng == "pe" or raw == "waw"):
                continue
            dl.append(x)
        if extra_deps:
            dl.extend(extra_deps)
        o.deps = dl
        o.inc = False
        for k in r:
            self.readers.setdefault(k, []).append(o)
        for k in w:
            self.last_w[k] = o
            self.readers[k] = []
        self.ops[eng].append(o)
        self.allops.append(o)
        self.chan_ops.setdefault(o.chan, []).append(o)
        o.seq = len(self.chan_ops[o.chan])
        return o

    def dma(self, eng, out, in_, r=(), w=(), sem=None, **kw):
        return self.add(eng, lambda e: e.dma_start(out=out, in_=in_, **kw), r=r, w=w, dma=sem)

    def barrier(self):
        lasts = [lst[-1] for lst in self.chan_ops.values() if lst]
        for e in ENGS:
            self.add(e, lambda eng: eng.nop(), extra_deps=list(lasts))
        self.last_w = {}
        self.readers = {}

    def emit(self):
        nc = self.nc
        known = {e: {} for e in ENGS}
        for o in self.allops:
            kn = known[o.eng]
            waits = []
            for x in sorted(o.deps, key=lambda x: -x.seq):
                if kn.get(x.chan, 0) >= x.seq:
                    continue
                waits.append(x)
                x.inc = True
                kn[x.chan] = x.seq
                for c, v in x.known.items():
                    if kn.get(c, 0) < v:
                        kn[c] = v
            o.waits = waits
            o.known = dict(kn)
        for chan, lst in self.chan_ops.items():
            c = 0
            for o in lst:
                if o.is_dma:
                    o.inc = True
                if o.inc:
                    c += 16 if o.is_dma else 1
                o.count = c
        for chan in self.chan_ops:
            nm = "s_" + "".join(ch if ch.isalnum() else "_" for ch in str(chan))
            self.sems[chan] = self.stack.enter_context(nc.semaphore(nm))
        engmap = {"pe": "tensor", "act": "scalar", "dve": "vector", "pool": "gpsimd", "sp": "sync"}
        self.n_waits = 0
        with nc.Block() as block:
            for e in ENGS:
                ops = self.ops[e]
                if not ops:
                    continue

                def body(eng, ops=ops):
                    for o in ops:
                        for x in o.waits:
                            eng.wait_ge(self.sems[x.chan], x.count)
                            self.n_waits += 1
                        ins = o.fn(eng)
                        if o.inc:
                            ins.then_inc(self.sems[o.chan], 16 if o.is_dma else 1)

                getattr(block, engmap[e])(body)


def emit_skewed(jobs, skew):
    n = len(jobs)
    for i in range(n + skew):
        if i < n:
            jobs[i][0]()
        if i - skew >= 0:
            jobs[i - skew][1]()


class Rot:
    def __init__(self, items):
        self.items = items
        self.i = 0

    def next(self):
        it = self.items[self.i % len(self.items)]
        self.i += 1
        return it


def rope_tables(dim, reps):
    nf = dim // 4
    t = np.arange(NX)
    row = (t // 64).astype(np.float32)
    col = (t % 64).astype(np.float32)
    inv = (10000.0 ** (-np.arange(nf, dtype=np.float32) / nf)).astype(np.float32)
    ar = row[:, None] * inv
    ac = col[:, None] * inv
    ang = np.concatenate([ar, ar, ac, ac], axis=-1).astype(np.float32)
    cos = np.cos(ang).astype(np.float32).T
    sin = np.sin(ang).astype(np.float32).T
    sp = np.zeros((dim, dim), np.float32)
    for i in range(dim):
        q = i // nf
        if q % 2 == 0:
            sp[i + nf, i] = -1.0
        else:
            sp[i - nf, i] = 1.0
    cosr = np.tile(cos, (reps, 1))
    sinr = np.tile(sin, (reps, 1))
    spr = np.kron(np.eye(reps, dtype=np.float32), sp)
    return np.ascontiguousarray(cosr), np.ascontiguousarray(sinr), np.ascontiguousarray(spr)


def gqa_perms():
    qperm = np.zeros(1024, np.int64)
    for jj in range(2):
        for g in range(4):
            for half in range(2):
                head = (2 * jj + half) * 4 + g
                cq = jj * 4 + g
                qperm[cq * 128 + half * 64: cq * 128 + half * 64 + 64] = head * 64 + np.arange(64)
    operm = np.zeros(1024, np.int64)
    for kvh in range(4):
        for gp in range(2):
            for half in range(2):
                head = kvh * 4 + gp + 2 * half
                c = kvh * 2 + gp
                operm[c * 128 + half * 64: c * 128 + half * 64 + 64] = head * 64 + np.arange(64)
    return qperm, operm


def build_program(layers, nseq, final):
    nc = bass.Bass("TRN2", target_bir_lowering=False)
    NL = len(layers)

    def din(name, shape):
        return nc.dram_tensor(name, list(shape), F32, kind="ExternalInput").ap()

    def dout(name, shape):
        return nc.dram_tensor(name, list(shape), F32, kind="ExternalOutput").ap()

    x_d = din("x", [nseq, NX, D])
    ctx_d = din("ctx", [nseq, NCTX, D])
    cond_d = din("condT", [128, 8, nseq + 1])
    adaw_d = din("ada_w", [DEPTH, D, 3 * D])
    adab_d = din("ada_bT", [DEPTH, 128, 24])
    ng_d = din("norm_gT", [DEPTH, 128, 8])
    fg_d = din("final_gT", [128, 8])
    ident_d = din("ident", [128, 128])
    ones_d = din("ones", [128, 128])
    has_a = any(l % 3 == 0 for l in layers)
    has_b = any(l % 3 == 1 for l in layers)
    has_c = any(l % 3 == 2 for l in layers)
    if has_a:
        a_win_d = din("a_w_in", [2, D, 2560])
        a_sink_d = din("a_sink_bc", [2, 128, 16])
        a_wout_d = din("a_w_out", [2, D, D])
        ropeA_cos_d = din("ropeA_cos", [128, NX])
        ropeA_sin_d = din("ropeA_sin", [128, NX])
        spA_d = din("spA", [128, 128])
        mprev_d = din("mask_prev", [128, 128])
        mnext_d = din("mask_next", [128, 128])
    if has_b:
        b_win_d = din("b_w_in", [1, D, 1824])
        b_wuq_d = din("b_w_uq", [1, 512, 1536])
        b_wukv_d = din("b_w_ukv", [1, 256, 2048])
        b_wout_d = din("b_w_out", [1, D, D])
        b_qn_d = din("b_q_normT", [1, 128, 4])
        b_kvn_d = din("b_kv_normT", [1, 128, 2])
        ropeB_cos_d = din("ropeB_cos", [128, NX])
        ropeB_sin_d = din("ropeB_sin", [128, NX])
        spB_d = din("spB", [128, 128])
    if has_c:
        c_win_d = din("c_w_in", [1, D, 5184])
        c_wout_d = din("c_w_out", [1, 2048, D])
        c_cwT_d = din("c_conv_wT", [128, 24, 3])
        c_cbT_d = din("c_conv_bT", [128, 24])
        c_dtb_d = din("c_dt_bias_bc", [128, 64])
        c_alog_d = din("c_a_log_bc", [128, 64])
        c_dsk_d = din("c_d_bc", [128, 32])
        c_nwT_d = din("c_normT", [128, 16])
        triU_d = din("triU", [128, 128])
        triL_d = din("triL", [128, 128])
        triSL_d = din("triSL", [128, 128])
        triSU_d = din("triSU", [128, 128])
    if final:
        out_d = dout("out", [nseq, NX, D])
    else:
        xo_d = dout("x_out", [nseq, NX, D])
        co_d = dout("ctx_out", [nseq, NCTX, D])

    st = ExitStack()
    with st:
        P = Prog(nc, st)
        R = P.sbuf("R", [128, 8, NT], F32)
        hT = P.sbuf("hT", [128, 8, NT], BF16)
        mod = P.sbuf("mod", [128, NL, 24, nseq + 1], F32)
        Acol = P.sbuf("Acol", [128, 8, 2], F32)
        ident_f = P.sbuf("ident_f", [128, 128], F32)
        ident_b = P.sbuf("ident_b", [128, 128], BF16)
        ones_b = P.sbuf("ones_b", [128, 128], BF16)
        ngT = P.sbuf("ngT", [128, DEPTH, 8], F32)
        fgT = P.sbuf("fgT", [128, 8], F32)
        adabT = P.sbuf("adabT", [128, DEPTH, 24], F32)
        condT = P.sbuf("condT_sb", [128, 8, nseq + 1], F32)
        scond = P.sbuf("scond", [128, 8, nseq + 1], BF16)
        WS = [P.sbuf(f"WS{i}", [128, 8, 512], BF16) for i in range(2)]
        tmpf = [P.sbuf(f"tmpf{i}", [128, 512], F32) for i in range(3)]
        tmpb = [P.sbuf(f"tmpb{i}", [128, 512], BF16) for i in range(3)]
        rstd = P.sbuf("rstd", [128, 512], F32)
        ps = [P.psum(f"ps{i}", [128, 512], F32) for i in range(8)]

        arena_box = []

        def carve_bf(off_bytes, nelem):
            assert off_bytes + nelem * 2 <= ARENA_BYTES[0], (off_bytes, nelem, ARENA_BYTES[0])
            return arena_box[0][:, off_bytes // 4: off_bytes // 4 + nelem // 2].bitcast(BF16)

        def carve_f(off_bytes, nelem):
            assert off_bytes + nelem * 4 <= ARENA_BYTES[0], (off_bytes, nelem, ARENA_BYTES[0])
            return arena_box[0][:, off_bytes // 4: off_bytes // 4 + nelem]

        ARENA_BYTES = [0]
        STAGE_OFF = 16 * 1024

        ws_rot = Rot([(WS[0], ("ws", 0)), (WS[1], ("ws", 1))])
        tmpf_rot = Rot([(tmpf[i], ("tmpf", i)) for i in range(3)])
        tmpb_rot = Rot([(tmpb[i], ("tmpb", i)) for i in range(3)])

        def tiles_of(tok0, ntok):
            return range(tok0 // 128, (tok0 + ntok) // 128)

        XBLK = [(NCTX + 512 * j, 512) for j in range(4)]
        CBLK = (0, NCTX)

        def MM(out, lhsT, rhs, start, stop, r, w, **kw):
            P.add("pe", lambda e: e.matmul(out, lhsT=lhsT, rhs=rhs, start=start, stop=stop, **kw), r, w)

        def ACT(out, in_, func, r, w, **kw):
            P.add("act", lambda e: e.activation(out=out, in_=in_, func=func, **kw), r, w)

        def TT(out, in0, in1, op, r, w, eng="dve"):
            P.add(eng, lambda e: e.tensor_tensor(out=out, in0=in0, in1=in1, op=op), r, w)

        def TS(out, in0, s1, s2, op0, op1, r, w, eng="dve"):
            if s2 is None:
                P.add(eng, lambda e: e.tensor_scalar(out=out, in0=in0, scalar1=s1, scalar2=None, op0=op0), r, w)
            else:
                P.add(eng, lambda e: e.tensor_scalar(out=out, in0=in0, scalar1=s1, scalar2=s2, op0=op0, op1=op1), r, w)

        def STT(out, in0, scalar, in1, op0, op1, r, w):
            P.add("dve", lambda e: e.scalar_tensor_tensor(out=out, in0=in0, scalar=scalar, in1=in1, op0=op0, op1=op1), r, w)

        def CP(out, in_, r, w, eng="dve"):
            P.add(eng, lambda e: e.tensor_copy(out=out, in_=in_), r, w)

        def RECIP(out, in_, r, w):
            P.add("dve", lambda e: e.reciprocal(out=out, in_=in_), r, w)

        dma_ctr = [0]

        def DMA(eng, out, in_, r, w, sem):
            P.dma(eng, out, in_, r=r, w=w, sem=sem)

        def load_w(dst, dst_key, src_ap, sem):
            DMA("pool", dst, src_ap, r=[], w=[dst_key], sem=sem)

        DMA("sp", ident_f[:], ident_d, [], ["ident_f"], "c_identf")
        DMA("pool", ident_b[:], ident_d, [], ["ident_b"], "c_identb")
        DMA("pool", ones_b[:], ones_d, [], ["ones_b"], "c_onesb")
        DMA("sp", ngT[:], ng_d.rearrange("l p c -> p l c"), [], ["ngT"], "c_ng")
        DMA("sp", fgT[:], fg_d, [], ["fgT"], "c_fg")
        DMA("sp", adabT[:], adab_d.rearrange("l p c -> p l c"), [], ["adabT"], "c_adab")
        DMA("sp", condT[:], cond_d, [], ["condT"], "c_cond")
        ACT(scond[:], condT[:], AF.Silu, ["condT"], ["scond"])

        for li, l in enumerate(layers):
            for cg in range(6):
                wsb, wkey = ws_rot.next()
                load_w(wsb[:], wkey, adaw_d[l, :, cg * 512:(cg + 1) * 512].rearrange("(kc p) c -> p kc c", p=128),
                       sem=("ws", wkey[1]))
                for cc in range(4):
                    j = cg * 4 + cc
                    pst = ps[j % 2]
                    for kc in range(8):
                        MM(pst[:, 0:nseq + 1], wsb[:, kc, cc * 128:(cc + 1) * 128], scond[:, kc, :],
                           kc == 0, kc == 7, [wkey, "scond"], [("ps", j % 2)])
                    TS(mod[:, li, j, :], pst[:, 0:nseq + 1], adabT[:, l, j:j + 1], None, ALU.add, None,
                       [("ps", j % 2), "adabT"], [("mod", li)])

        def load_seq(s):
            for t in range(NTILE):
                stg, skey = stage_rot.next()
                src = ctx_d[s, t * 128:(t + 1) * 128, :] if t < 2 else x_d[s, (t - 2) * 128:(t - 1) * 128, :]
                DMA("sp", stg[:], src, [], [skey], ("stage", skey[1]))
                for half in range(2):
                    pst = ps[(2 * t + half) % 8]
                    pkey = ("ps", (2 * t + half) % 8)
                    for q in range(4):
                        dc = half * 4 + q
                        P.add("pe", lambda e, o=pst[:, q * 128:(q + 1) * 128], i=stg[:, dc * 128:(dc + 1) * 128]:
                              e.transpose(o, i, ident_f[:]), [skey, "ident_f"], [pkey])
                    eng = "dve" if half == 0 else "act"
                    outap = R[:, half * 4:half * 4 + 4, t * 128:(t + 1) * 128]
                    inap = pst[:].rearrange("p (a b) -> p a b", b=128)
                    if eng == "dve":
                        CP(outap, inap, [pkey], [("R", dc, t) for dc in range(half * 4, half * 4 + 4)])
                    else:
                        ACT(outap, inap, AF.Copy, [pkey], [("R", dc, t) for dc in range(half * 4, half * 4 + 4)])

        def store_seq(s, normed, tiles=None, fin_t0=0):
            if tiles is None:
                tiles = range(NTILE)
            for t in tiles:
                stg, skey = stage_rot.next()
                for half in range(2):
                    pst = ps[(2 * t + half) % 8]
                    pkey = ("ps", (2 * t + half) % 8)
                    for q in range(4):
                        dc = half * 4 + q
                        src = normed[:, dc, (t - fin_t0) * 128:(t - fin_t0 + 1) * 128] if final else R[:, dc, t * 128:(t + 1) * 128]
                        rk = [("fin", dc, t)] if final else [("R", dc, t)]
                        P.add("pe", lambda e, o=pst[:, q * 128:(q + 1) * 128], i=src: e.transpose(o, i, ident_f[:]),
                              rk + ["ident_f"], [pkey])
                    if half == 0:
                        CP(stg[:, 0:512], pst[:], [pkey], [skey])
                    else:
                        ACT(stg[:, 512:1024], pst[:], AF.Copy, [pkey], [skey])
                if final:
                    dst = out_d[s, (t - 2) * 128:(t - 1) * 128, :]
                elif t < 2:
                    dst = co_d[s, t * 128:(t + 1) * 128, :]
                else:
                    dst = xo_d[s, (t - 2) * 128:(t - 1) * 128, :]
                DMA("sp", dst, stg[:], [skey], [("out", s, t)], ("stage", skey[1]))
                out_keys.append(("out", s, t))

        out_keys = []

        def norm_block(tok0, ntok, a_ap, b_ap, dst_fn, dst_keyf, akeys):
            tl = list(tiles_of(tok0, ntok))
            pst, pkey = ps[7], ("ps", 7)
            for dc in range(8):
                sqb, sqk = tmpb_rot.next()
                ACT(sqb[:, 0:ntok], R[:, dc, tok0:tok0 + ntok], AF.Square, [("R", dc, t) for t in tl], [sqk])
                MM(pst[:, 0:ntok], ones_b[:], sqb[:, 0:ntok], dc == 0, dc == 7, [sqk, "ones_b"], [pkey])
            ACT(rstd[:, 0:ntok], pst[:, 0:ntok], AF.Sqrt, [pkey], ["rstd"], scale=1.0 / D, bias=EPS)
            RECIP(rstd[:, 0:ntok], rstd[:, 0:ntok], ["rstd"], ["rstd"])
            for dc in range(8):
                tf, tk = tmpf_rot.next()
                STT(tf[:, 0:ntok], R[:, dc, tok0:tok0 + ntok], a_ap[:, dc:dc + 1], rstd[:, 0:ntok], ALU.mult, ALU.mult,
                    [("R", dc, t) for t in tl] + ["rstd"] + akeys, [tk])
                wk = [dst_keyf(dc, t) for t in tl]
                if b_ap is not None:
                    ACT(dst_fn(dc), tf[:, 0:ntok], AF.Identity, [tk] + akeys, wk, bias=b_ap[:, dc:dc + 1], scale=1.0)
                else:
                    ACT(dst_fn(dc), tf[:, 0:ntok], AF.Copy, [tk], wk)

        def layer_norm_phase(li, l, s):
            for ci, cond in enumerate((s, nseq)):
                TS(tmpf[0][:, 0:8], mod[:, li, 8:16, cond], 1.0, None, ALU.add, None, [("mod", li)], [("tmpf", 0)])
                TT(Acol[:, :, ci], tmpf[0][:, 0:8], ngT[:, l, :], ALU.mult, [("tmpf", 0), "ngT"], ["Acol"])
            norm_block(CBLK[0], CBLK[1], Acol[:, :, 1], mod[:, li, 0:8, nseq], lambda dc: hT[:, dc, 0:NCTX],
                       lambda dc, t: ("h", t), ["Acol", ("mod", li)])
            for (t0, n) in XBLK:
                norm_block(t0, n, Acol[:, :, 0], mod[:, li, 0:8, s], lambda dc, t0=t0, n=n: hT[:, dc, t0:t0 + n],
                           lambda dc, t: ("h", t), ["Acol", ("mod", li)])

        def linear_T(wsb, wkey, col0, ncols, tok0, ntok, pst, pkey, rhs_buf=None, rhs_keyf=None, kcs=8):
            rb = hT if rhs_buf is None else rhs_buf
            kf = (lambda kc, t: ("h", t)) if rhs_keyf is None else rhs_keyf
            tl = list(tiles_of(tok0, ntok))
            for kc in range(kcs):
                MM(pst[0:ncols, 0:ntok], wsb[:, kc, col0:col0 + ncols], rb[:, kc, tok0:tok0 + ntok],
                   kc == 0, kc == kcs - 1, [wkey] + [kf(kc, t) for t in tl], [pkey])

        def residual_update(li, s, wsb, wkey, dcl, src_buf, src_keyf, need_ctx, psrot):
            blks = ([CBLK] if need_ctx else []) + XBLK
            for cc, dc in enumerate(dcl):
                for (t0, n) in blks:
                    pst, pkey = psrot.next()
                    tl = list(tiles_of(t0, n))
                    for kc in range(8):
                        MM(pst[:, 0:n], wsb[:, kc, cc * 128:(cc + 1) * 128], src_buf[:, kc, t0:t0 + n],
                           kc == 0, kc == 7, [wkey] + [src_keyf(kc, t) for t in tl], [pkey])
                    cond = nseq if t0 == 0 else s
                    rk = [("R", dc, t) for t in tl]
                    STT(R[:, dc, t0:t0 + n], pst[:, 0:n], mod[:, li, 16 + dc, cond:cond + 1], R[:, dc, t0:t0 + n],
                        ALU.mult, ALU.add, [pkey, ("mod", li)] + rk, rk)

        if has_a:
            spA = P.sbuf("spA_sb", [128, 128], BF16)
            mprev = P.sbuf("mprev_sb", [128, 128], BF16)
            mnext = P.sbuf("mnext_sb", [128, 128], BF16)
            esink = P.sbuf("esink_sb", [128, 2, 16], F32)
            DMA("pool", spA[:], spA_d, [], ["spA"], "c_spA")
            DMA("pool", mprev[:], mprev_d, [], ["mprev"], "c_mp")
            DMA("pool", mnext[:], mnext_d, [], ["mnext"], "c_mn")
            DMA("sp", esink[:], a_sink_d.rearrange("j p h -> p j h"), [], ["esink"], "c_sink")
            ACT(esink[:], esink[:], AF.Exp, ["esink"], ["esink"])

        def gqa_layer(li, l, s):
            j = l // 3
            need_ctx = l < DEPTH - 1
            QO = carve_bf(0, 8 * NT).rearrange("p (c t) -> p c t", t=NT)
            KT = carve_bf(36 * 1024, 2 * NT).rearrange("p (c t) -> p c t", t=NT)
            Vb = carve_bf(45 * 1024, NTILE * 384).rearrange("p (t c) -> p t c", c=384)
            VOFF = (0, 64, 192, 256)
            ropeA_cos = carve_bf(45 * 1024, NX)
            ropeA_sin = carve_bf(45 * 1024 + 4096, NX)
            DMA("pool", ropeA_cos, ropeA_cos_d, [], ["ropeA"], "c_rc")
            DMA("pool", ropeA_sin, ropeA_sin_d, [], ["ropeA"], "c_rs")
            layer_norm_phase(li, l, s)
            psrot = Rot([(ps[i], ("ps", i)) for i in (2, 3, 4, 5)])
            ps2rot = Rot([(ps[i], ("ps", i)) for i in (6, 7)])

            def rope_evac(pst, pkey, t0, n, dst, dkeys):
                if t0 < NCTX:
                    ACT(dst, pst[:, 0:n], AF.Copy, [pkey], dkeys)
                    return
                x0 = t0 - NCTX
                qraw, qk = tmpb_rot.next()
                ACT(qraw[:, 0:n], pst[:, 0:n], AF.Copy, [pkey], [qk])
                p2, p2k = ps2rot.next()
                MM(p2[:, 0:n], spA[:], qraw[:, 0:n], True, True, ["spA", qk], [p2k])
                t1, t1k = tmpf_rot.next()
                t2, t2k = tmpf_rot.next()
                TT(t1[:, 0:n], qraw[:, 0:n], ropeA_cos[:, x0:x0 + n], ALU.mult, [qk, "ropeA"], [t1k])
                TT(t2[:, 0:n], p2[:, 0:n], ropeA_sin[:, x0:x0 + n], ALU.mult, [p2k, "ropeA"], [t2k])
                TT(dst, t1[:, 0:n], t2[:, 0:n], ALU.add, [t1k, t2k], dkeys)

            for cg in range(2):
                wsb, wkey = ws_rot.next()
                load_w(wsb[:], wkey, a_win_d[j, :, cg * 512:(cg + 1) * 512].rearrange("(kc p) c -> p kc c", p=128),
                       sem=("ws", wkey[1]))
                for cc in range(4):
                    cq = cg * 4 + cc
                    for (t0, n) in ([CBLK] if need_ctx else []) + XBLK:
                        pst, pkey = psrot.next()
                        linear_T(wsb, wkey, cc * 128, 128, t0, n, pst, pkey)
                        rope_evac(pst, pkey, t0, n, QO[:, cq, t0:t0 + n], [("q", cq, t) for t in tiles_of(t0, n)])
            wsb, wkey = ws_rot.next()
            load_w(wsb[:], wkey, a_win_d[j, :, 1024:1536].rearrange("(kc p) c -> p kc c", p=128), sem=("ws", wkey[1]))
            for cc in range(2):
                for (t0, n) in [CBLK] + XBLK:
                    pst, pkey = psrot.next()
                    linear_T(wsb, wkey, cc * 128, 128, t0, n, pst, pkey)
                    rope_evac(pst, pkey, t0, n, KT[:, cc, t0:t0 + n], [("k", cc, t) for t in tiles_of(t0, n)])
            P.add("dve", lambda e: e.memset(Vb[:, :, 64:128], 1.0), [], ["vones", "ropeA"])
            P.add("dve", lambda e: e.memset(Vb[:, :, 256:320], 1.0), [], ["vones", "ropeA"])
            for t in range(NTILE):
                pst, pkey = psrot.next()
                for kc in range(8):
                    MM(pst[:, 0:256], hT[:, kc, t * 128:(t + 1) * 128], wsb[:, kc, 256:512], kc == 0, kc == 7,
                       [wkey, ("h", t)], [pkey])
                CP(Vb[:, t, 0:64], pst[:, 0:64], [pkey, "vones"], [("v", t)])
                CP(Vb[:, t, 128:256], pst[:, 64:192], [pkey], [("v", t)])
                CP(Vb[:, t, 320:384], pst[:, 192:256], [pkey], [("v", t)])

            PT_OFF = 45 * 1024 + NTILE * 384 * 2
            ptrot = Rot([(carve_bf(PT_OFF + i * 1024, 512), ("pt", i)) for i in range(5)] + [(tmpb[0][:], ("tmpb", 0))])
            sc_rot = Rot([(ps[i], ("ps", i)) for i in (4, 5, 6, 7)])
            tiles = []
            jobno = [0]

            def add_qtile(T, key_tiles):
                for jj in range(2):
                    pair = []
                    for half in range(2):
                        ob = jobno[0] % 4
                        jobno[0] += 1
                        pair.append((half, ps[ob], ("ps", ob)))
                        for ki, (kt, msk) in enumerate(key_tiles):
                            tiles.append(dict(T=T, jj=jj, half=half, kt=kt, msk=msk, first=(ki == 0),
                                              last=(ki == len(key_tiles) - 1), OP=ps[ob], OPk=("ps", ob), pair=pair))

            for qt in range(16):
                kts = [(0, None), (1, None)]
                if qt > 0:
                    kts.append((2 + qt - 1, (mprev, "mprev")))
                kts.append((2 + qt, None))
                if qt < 15:
                    kts.append((2 + qt + 1, (mnext, "mnext")))
                add_qtile(2 + qt, kts)
            if need_ctx:
                for T in range(2):
                    add_qtile(T, [(0, None), (1, None)])

            def att_s1(tl_):
                T, jj, half, kt = tl_["T"], tl_["jj"], tl_["half"], tl_["kt"]
                base = 64 * half
                sp_, spk = sc_rot.next()
                MM(sp_[:].rearrange("p (a b) -> p a b", b=128), KT[base:base + 64, jj, kt * 128:(kt + 1) * 128],
                   QO[base:base + 64, jj * 4:jj * 4 + 4, T * 128:(T + 1) * 128], True, True,
                   [("q", jj * 4 + g, T) for g in range(4)] + [("k", jj, kt)], [spk])
                pt, ptk = ptrot.next()
                ACT(pt, sp_[:], AF.Exp, [spk], [ptk], scale=0.125)
                if tl_["msk"] is not None:
                    msk = tl_["msk"]
                    ptv = pt.rearrange("p (a b) -> p a b", b=128)
                    TT(ptv, ptv, msk[0][:].unsqueeze(1).to_broadcast([128, 4, 128]), ALU.mult, [ptk, msk[1]], [ptk])
                tl_["pt"] = (pt, ptk)

            def att_s2(tl_):
                T, jj, half, kt = tl_["T"], tl_["jj"], tl_["half"], tl_["kt"]
                kvh = 2 * jj + half
                pt, ptk = tl_["pt"]
                MM(tl_["OP"][:], Vb[:, kt, VOFF[kvh]:VOFF[kvh] + 128], pt, tl_["first"], tl_["last"],
                   [ptk, ("v", kt), "vones"], [tl_["OPk"]])
                if not (tl_["last"] and half == 1):
                    return
                for (hf, OP, OPk) in tl_["pair"]:
                    kv = 2 * jj + hf
                    ob, db = 64 * hf, 64 * (1 - hf)
                    den, denk = tmpf_rot.next()
                    CP(den[ob:ob + 64, :], OP[db:db + 64, :], [OPk], [denk])
                    for g in range(4):
                        h = kv * 4 + g
                        TS(den[ob:ob + 64, g * 128:(g + 1) * 128], den[ob:ob + 64, g * 128:(g + 1) * 128],
                           esink[ob:ob + 64, j, h:h + 1], None, ALU.add, None, [denk, "esink"], [denk])
                    ACT(den[ob:ob + 64, :], den[ob:ob + 64, :], AF.Ln, [denk], [denk])
                    ACT(den[ob:ob + 64, :], den[ob:ob + 64, :], AF.Exp, [denk], [denk], scale=-1.0)
                    TT(QO[ob:ob + 64, jj * 4:jj * 4 + 4, T * 128:(T + 1) * 128],
                       OP[ob:ob + 64, :].rearrange("p (a b) -> p a b", b=128),
                       den[ob:ob + 64, :].rearrange("p (a b) -> p a b", b=128), ALU.mult, [OPk, denk],
                       [("q", jj * 4 + g, T) for g in range(4)])

            G = 3
            groups = [tiles[i:i + G] for i in range(0, len(tiles), G)]
            emit_skewed([((lambda gr=gr: [att_s1(t_) for t_ in gr]), (lambda gr=gr: [att_s2(t_) for t_ in gr]))
                         for gr in groups], 1)

            blks = ([CBLK] if need_ctx else []) + XBLK
            for cg in range(2):
                wsb, wkey = ws_rot.next()
                load_w(wsb[:], wkey, a_win_d[j, :, 1536 + cg * 512:1536 + (cg + 1) * 512].rearrange("(kc p) c -> p kc c", p=128),
                       sem=("ws", wkey[1]))
                for cc in range(4):
                    c = cg * 4 + cc
                    for (t0, n) in blks:
                        pst, pkey = psrot.next()
                        linear_T(wsb, wkey, cc * 128, 128, t0, n, pst, pkey)
                        sg, sgk = tmpb_rot.next()
                        ACT(sg[:, 0:n], pst[:, 0:n], AF.Silu, [pkey], [sgk])
                        qk = [("q", c, t) for t in tiles_of(t0, n)]
                        TT(QO[:, c, t0:t0 + n], QO[:, c, t0:t0 + n], sg[:, 0:n], ALU.mult, [sgk] + qk, qk)
            for dg in range(2):
                wsb, wkey = ws_rot.next()
                load_w(wsb[:], wkey, a_wout_d[j, :, dg * 512:(dg + 1) * 512].rearrange("(kc p) c -> p kc c", p=128),
                       sem=("ws", wkey[1]))
                residual_update(li, s, wsb, wkey, [dg * 4 + q for q in range(4)], QO, lambda kc, t: ("q", kc, t),
                                need_ctx, psrot)


        if has_b:
            spB = P.sbuf("spB_sb", [128, 128], BF16)
            qnT = P.sbuf("qnT_sb", [128, 4], F32)
            kvnT = P.sbuf("kvnT_sb", [128, 2], F32)
            DMA("pool", spB[:], spB_d, [], ["spB"], "c_spB")
            DMA("sp", qnT[:], b_qn_d[0], [], ["qnT"], "c_qn")
            DMA("sp", kvnT[:], b_kvn_d[0], [], ["kvnT"], "c_kvn")

        def mla_layer(li, l, s):
            j = 0
            need_ctx = l < DEPTH - 1
            SC = 96.0 ** -0.5
            CQ = carve_bf(0, 4 * NT).rearrange("p (c t) -> p c t", t=NT)
            CKV = carve_bf(18 * 1024, 2 * NT).rearrange("p (c t) -> p c t", t=NT)
            Q2 = carve_bf(27 * 1024, 2 * NT).rearrange("p (h t) -> p h t", t=NT)
            K2 = carve_bf(27 * 1024 + 9216, 2 * NT).rearrange("p (h t) -> p h t", t=NT)
            OG = carve_bf(27 * 1024 + 4 * 4608, NT)
            Vp = carve_bf(27 * 1024 + 5 * 4608, NTILE * 192).rearrange("p (t c) -> p t c", c=192)
            P.add("dve", lambda e: e.memset(Vp[:, :, 64:128], 1.0), [], ["vones"])
            RB_OFF = 27 * 1024 + 5 * 4608 + NTILE * 192 * 2
            ropeB_cos = carve_bf(RB_OFF, NX)
            ropeB_sin = carve_bf(RB_OFF + 4096, NX)
            DMA("pool", ropeB_cos, ropeB_cos_d, [], ["ropeB"], "c_rc")
            DMA("pool", ropeB_sin, ropeB_sin_d, [], ["ropeB"], "c_rs")
            layer_norm_phase(li, l, s)
            psrot = Rot([(ps[i], ("ps", i)) for i in (2, 3, 4, 5)])
            ps2rot = Rot([(ps[i], ("ps", i)) for i in (6, 7)])
            qblks = ([CBLK] if need_ctx else []) + XBLK
            ablks = [CBLK] + XBLK

            def rope_hi(pst, pkey, t0, n, dsts_hi, dkeys, dst_lo=None, lokeys=None):
                if dst_lo is not None:
                    CP(dst_lo, pst[0:64, 0:n], [pkey], lokeys)
                if t0 < NCTX:
                    for d_ in dsts_hi:
                        ACT(d_, pst[64:128, 0:n], AF.Copy, [pkey], dkeys)
                    return
                x0 = t0 - NCTX
                qraw, qk = tmpb_rot.next()
                ACT(qraw[:, 0:n], pst[:, 0:n], AF.Copy, [pkey], [qk])
                p2, p2k = ps2rot.next()
                MM(p2[:, 0:n], spB[:], qraw[:, 0:n], True, True, ["spB", qk], [p2k])
                t1, t1k = tmpf_rot.next()
                t2, t2k = tmpf_rot.next()
                TT(t1[64:128, 0:n], qraw[64:128, 0:n], ropeB_cos[64:128, x0:x0 + n], ALU.mult, [qk, "ropeB"], [t1k])
                TT(t2[64:128, 0:n], p2[64:128, 0:n], ropeB_sin[64:128, x0:x0 + n], ALU.mult, [p2k, "ropeB"], [t2k])
                for d_ in dsts_hi:
                    TT(d_, t1[64:128, 0:n], t2[64:128, 0:n], ALU.add, [t1k, t2k], dkeys)

            def lora_norm(wsb, wkey, col0, nch, gcol, dst, dkey):
                for (t0, n) in ablks:
                    tl = list(tiles_of(t0, n))
                    banks = [(ps[c], ("ps", c)) for c in range(nch)]
                    for c in range(nch):
                        linear_T(wsb, wkey, col0 + c * 128, 128, t0, n, banks[c][0], banks[c][1])
                    pst, pkey = ps[7], ("ps", 7)
                    for c in range(nch):
                        sqb, sqk = tmpb_rot.next()
                        ACT(sqb[:, 0:n], banks[c][0][:, 0:n], AF.Square, [banks[c][1]], [sqk])
                        MM(pst[:, 0:n], ones_b[:], sqb[:, 0:n], c == 0, c == nch - 1, [sqk, "ones_b"], [pkey])
                    ACT(rstd[:, 0:n], pst[:, 0:n], AF.Sqrt, [pkey], ["rstd"], scale=1.0 / (nch * 128), bias=EPS)
                    RECIP(rstd[:, 0:n], rstd[:, 0:n], ["rstd"], ["rstd"])
                    for c in range(nch):
                        STT(dst[:, c, t0:t0 + n], banks[c][0][:, 0:n], gcol[:, c:c + 1], rstd[:, 0:n], ALU.mult, ALU.mult,
                            [banks[c][1], "rstd", "qnT", "kvnT"], [(dkey, c, t) for t in tl])

            wsb, wkey = ws_rot.next()
            load_w(wsb[:], wkey, b_win_d[j, :, 0:512].rearrange("(kc p) c -> p kc c", p=128), sem=("ws", wkey[1]))
            lora_norm(wsb, wkey, 0, 4, qnT, CQ, "cq")
            wsb, wkey = ws_rot.next()
            P.add("dve", lambda e, ap=wsb[:, :, 256:384]: e.memset(ap, 0.0), [], [wkey])
            load_w(wsb[:, :, 0:256], wkey, b_win_d[j, :, 512:768].rearrange("(kc p) c -> p kc c", p=128), sem=("ws", wkey[1]))
            load_w(wsb[:, :, 320:352], wkey, b_win_d[j, :, 768:800].rearrange("(kc p) c -> p kc c", p=128), sem=("ws", wkey[1]))
            lora_norm(wsb, wkey, 0, 2, kvnT, CKV, "ckv")
            for (t0, n) in ablks:
                pst, pkey = psrot.next()
                linear_T(wsb, wkey, 256, 128, t0, n, pst, pkey)
                rope_hi(pst, pkey, t0, n, [K2[64:128, 0, t0:t0 + n], K2[64:128, 1, t0:t0 + n]],
                        [("k2r", t) for t in tiles_of(t0, n)])

            for (wsb_, wkey_) in ws_rot.items:
                wq_ = wsb_[:].rearrange("p a b -> p (a b)")[:, 0:1024].rearrange("p (k c) -> p k c", c=256)
                for hh in range(2):
                    P.add("dve", lambda e, ap=wq_[:, :, hh * 128 + 96:hh * 128 + 128]: e.memset(ap, 0.0), [], [wkey_])
            for c in range(8):
                wsb, wkey = ws_rot.next()
                wflat = wsb[:].rearrange("p a b -> p (a b)")
                wq = wflat[:, 0:1024].rearrange("p (k c) -> p k c", c=256)
                wkv = wflat[:, 1024:1536].rearrange("p (k c) -> p k c", c=256)
                wg = wflat[:, 1536:2560].rearrange("p (k c) -> p k c", c=128)
                wo = wflat[:, 2560:3584]
                sem = ("ws", wkey[1])
                for hh in range(2):
                    h = 2 * c + hh
                    load_w(wq[:, :, hh * 128:hh * 128 + 96], wkey,
                           b_wuq_d[j, :, h * 96:(h + 1) * 96].rearrange("(kc p) c -> p kc c", p=128), sem=sem)
                load_w(wkv, wkey, b_wukv_d[j, :, c * 256:(c + 1) * 256].rearrange("(kc p) c -> p kc c", p=128), sem=sem)
                load_w(wg, wkey, b_win_d[j, :, 800 + c * 128:800 + (c + 1) * 128].rearrange("(kc p) c -> p kc c", p=128), sem=sem)
                load_w(wo, wkey, b_wout_d[j, c * 128:(c + 1) * 128, :], sem=sem)
                cqk = lambda kc, t: ("cq", kc, t)
                ckvk = lambda kc, t: ("ckv", kc, t)
                for (t0, n) in qblks:
                    tl = list(tiles_of(t0, n))
                    for hh in range(2):
                        pst, pkey = psrot.next()
                        linear_T(wq, wkey, hh * 128, 128, t0, n, pst, pkey, rhs_buf=CQ, rhs_keyf=cqk, kcs=4)
                        rope_hi(pst, pkey, t0, n, [Q2[64:128, hh, t0:t0 + n]], [("q2", hh, t) for t in tl],
                                dst_lo=Q2[0:64, hh, t0:t0 + n], lokeys=[("q2", hh, t) for t in tl])
                for (t0, n) in ablks:
                    tl = list(tiles_of(t0, n))
                    pst, pkey = psrot.next()
                    linear_T(wkv, wkey, 0, 128, t0, n, pst, pkey, rhs_buf=CKV, rhs_keyf=ckvk, kcs=2)
                    CP(K2[0:64, 0, t0:t0 + n], pst[0:64, 0:n], [pkey], [("kn", 0, t) for t in tl])
                    ACT(K2[0:64, 1, t0:t0 + n], pst[64:128, 0:n], AF.Copy, [pkey], [("kn", 1, t) for t in tl])
                for t in range(NTILE):
                    pst, pkey = psrot.next()
                    for kc in range(2):
                        MM(pst[:, 0:128], CKV[:, kc, t * 128:(t + 1) * 128], wkv[:, kc, 128:256], kc == 0, kc == 1,
                           [wkey, ("ckv", kc, t)], [pkey])
                    CP(Vp[:, t, 0:64], pst[:, 0:64], [pkey], [("v", t)])
                    CP(Vp[:, t, 128:192], pst[:, 64:128], [pkey], [("v", t)])
                jobs = []
                for (t0, n) in qblks:
                    tl = list(tiles_of(t0, n))
                    kts = [0, 1] if t0 < NCTX else list(range(NTILE))
                    for hh in range(2):
                        OP, OPk = ps[hh], ("ps", hh)
                        for ki, kt in enumerate(kts):
                            st = {}

                            def s1(kt=kt, t0=t0, n=n, tl=tl, hh=hh, st=st):
                                sp_, spk = psrot.next()
                                MM(sp_[:, 0:n], K2[:, hh, kt * 128:(kt + 1) * 128], Q2[:, hh, t0:t0 + n], True, True,
                                   [("kn", hh, kt), ("k2r", kt)] + [("q2", hh, t) for t in tl], [spk])
                                pt, ptk = tmpb_rot.next()
                                ACT(pt[:, 0:n], sp_[:, 0:n], AF.Exp, [spk], [ptk], scale=SC)
                                st["pt"] = (pt, ptk)

                            def s2(kt=kt, ki=ki, nk=len(kts), t0=t0, n=n, tl=tl, hh=hh, OP=OP, OPk=OPk, st=st):
                                pt, ptk = st["pt"]
                                voff = 64 * hh
                                MM(OP[:, 0:n], Vp[:, kt, voff:voff + 128], pt[:, 0:n], ki == 0, ki == nk - 1,
                                   [ptk, ("v", kt), "vones"], [OPk])
                                if ki == nk - 1:
                                    ob, db = 64 * hh, 64 * (1 - hh)
                                    den, denk = tmpf_rot.next()
                                    CP(den[ob:ob + 64, 0:n], OP[db:db + 64, 0:n], [OPk], [denk])
                                    RECIP(den[ob:ob + 64, 0:n], den[ob:ob + 64, 0:n], [denk], [denk])
                                    TT(OG[ob:ob + 64, t0:t0 + n], OP[ob:ob + 64, 0:n], den[ob:ob + 64, 0:n], ALU.mult,
                                       [OPk, denk], [("og", t) for t in tl])

                            jobs.append((s1, s2))
                emit_skewed(jobs, 2)
                gjobs = []
                for (t0, n) in qblks:
                    def g1(t0=t0, n=n):
                        tl = list(tiles_of(t0, n))
                        pst, pkey = psrot.next()
                        linear_T(wg, wkey, 0, 128, t0, n, pst, pkey)
                        sg, sgk = tmpb_rot.next()
                        ACT(sg[:, 0:n], pst[:, 0:n], AF.Silu, [pkey], [sgk])
                        ogk = [("og", t) for t in tl]
                        TT(OG[:, t0:t0 + n], OG[:, t0:t0 + n], sg[:, 0:n], ALU.mult, [sgk] + ogk, ogk)

                    def g2(t0=t0, n=n):
                        tl = list(tiles_of(t0, n))
                        ogk = [("og", t) for t in tl]
                        cond = nseq if t0 == 0 else s
                        for dc in range(8):
                            pst, pkey = psrot.next()
                            MM(pst[:, 0:n], wo[:, dc * 128:(dc + 1) * 128], OG[:, t0:t0 + n], True, True, [wkey] + ogk, [pkey])
                            rk = [("R", dc, t) for t in tl]
                            STT(R[:, dc, t0:t0 + n], pst[:, 0:n], mod[:, li, 16 + dc, cond:cond + 1], R[:, dc, t0:t0 + n],
                                ALU.mult, ALU.add, [pkey, ("mod", li)] + rk, rk)

                    gjobs.append((g1, g2))
                emit_skewed(gjobs, 1)

        if has_c:
            triU = P.sbuf("triU_sb", [128, 128], F32)
            triL = P.sbuf("triL_sb", [128, 128], F32)
            triSL = P.sbuf("triSL_sb", [128, 128], F32)
            triSU = P.sbuf("triSU_sb", [128, 128], F32)
            ones_f = P.sbuf("ones_f_sb", [128, 128], F32)
            cwT = P.sbuf("cwT_sb", [128, 24, 3], F32)
            cbT = P.sbuf("cbT_sb", [128, 24], F32)
            dtb = P.sbuf("dtb_sb", [128, 64], F32)
            Aneg = P.sbuf("Aneg_sb", [128, 64], F32)
            dsk = P.sbuf("dsk_sb", [128, 32], F32)
            nwT = P.sbuf("nwT_sb", [128, 16], F32)
            for (dst, src, nm) in ((triU, triU_d, "triU"), (triL, triL_d, "triL"), (triSL, triSL_d, "triSL"),
                                   (triSU, triSU_d, "triSU"), (ones_f, ones_d, "ones_f"), (cwT, c_cwT_d, "cwT"),
                                   (cbT, c_cbT_d, "cbT"), (dtb, c_dtb_d, "dtb"), (Aneg, c_alog_d, "Aneg"),
                                   (dsk, c_dsk_d, "dsk"), (nwT, c_nwT_d, "nwT")):
                DMA("sp", dst[:], src, [], [nm], "c_" + nm)
            ACT(Aneg[:], Aneg[:], AF.Exp, ["Aneg"], ["Aneg"])
            TS(Aneg[:], Aneg[:], -1.0, None, ALU.mult, None, ["Aneg"], ["Aneg"])

        def ssd_layer(li, l, s):
            j = 0
            need_ctx = l < DEPTH - 1
            HP, NF, NCH = 8, 512, 4
            o = [0]

            def take_bf(nelem):
                ap = carve_bf(o[0], nelem)
                o[0] += nelem * 2
                return ap

            def take_f(nelem):
                ap = carve_f(o[0], nelem)
                o[0] += nelem * 4
                return ap

            xsT = take_bf(NCH * NT).rearrange("p (c t) -> p c t", t=NT)
            BT = take_bf(NT)
            CT = take_bf(NT)
            YP = take_bf(NTILE * NF).rearrange("p (t c) -> p t c", c=NF)
            dtt = take_f(NTILE * 2 * HP).rearrange("p (t c) -> p t c", c=2 * HP)
            _S1 = take_f(NF)
            _Sb1 = take_bf(NF)
            Sst = [_S1, _S1]
            Sb = [_Sb1, _Sb1]
            xs_tok2 = [take_bf(NF) for _ in range(2)]
            B_tok2 = [take_bf(128) for _ in range(2)]
            Xdt = take_bf(NF)
            Xds = take_bf(NF)
            Mt = take_bf(HP * 128).rearrange("p (r l) -> p r l", l=128)
            CBm2 = [[take_bf(128) for _ in range(2)] for _ in range(2)]
            sm = take_f(8 * 2 * HP)
            yacc = take_f(NF)
            yT = take_bf(NCH * 512).rearrange("p (c t) -> p c t", t=512)
            assert 27 * 1024 + 2 * NT * 4 <= o[0]
            layer_norm_phase(li, l, s)
            psrot = Rot([(ps[i], ("ps", i)) for i in (1, 7)])
            ablks = [CBLK] + XBLK
            SEGS = [(0, NCTX), (NCTX, NT)]

            raws = [carve_f(27 * 1024 + i * NT * 4, NT) for i in range(2)]
            raw_i = [0]

            def conv_chunk(wsb, wkey, col0, ch, dst_fn, dkeyf):
                ri = raw_i[0] % 2
                raw_i[0] += 1
                raw = raws[ri]
                rkey = lambda t: ("raw", ri, t)
                ypk = [("YP", t) for t in range(9 * ri, 9 * ri + 9)]
                for (t0, n) in ablks:
                    pst, pkey = psrot.next()
                    linear_T(wsb, wkey, col0, 128, t0, n, pst, pkey)
                    ACT(raw[:, t0:t0 + n], pst[:, 0:n], AF.Copy, [pkey], [rkey(t) for t in tiles_of(t0, n)] + ypk)
                jobs = []
                for (t0, n) in ablks:
                    st = {}

                    def c1(t0=t0, n=n, st=st):
                        tl = list(tiles_of(t0, n))
                        acc, acck = tmpf_rot.next()
                        st["acc"] = (acc, acck)
                        ACT(acc[:, 0:n], raw[:, t0:t0 + n], AF.Identity, [rkey(t) for t in tl] + ["cwT", "cbT"] + ypk, [acck],
                            scale=cwT[:, ch, 1:2], bias=cbT[:, ch:ch + 1])

                    def c2(t0=t0, n=n, st=st):
                        tl = list(tiles_of(t0, n))
                        acc, acck = st["acc"]
                        seg0, seg1 = (0, NCTX) if t0 < NCTX else (NCTX, NT)
                        lo = max(t0, seg0 + 1)
                        rk = [rkey(t) for t in range(max(t0 - 1, 0) // 128, min(t0 + n + 1, NT - 1) // 128 + 1)]
                        STT(acc[:, lo - t0:n], raw[:, lo - 1:t0 + n - 1], cwT[:, ch, 0:1], acc[:, lo - t0:n], ALU.mult, ALU.add,
                            rk + [acck, "cwT"] + ypk, [acck])
                        hi = min(t0 + n, seg1 - 1)
                        STT(acc[:, 0:hi - t0], raw[:, t0 + 1:hi + 1], cwT[:, ch, 2:3], acc[:, 0:hi - t0], ALU.mult, ALU.add,
                            rk + [acck, "cwT"] + ypk, [acck])
                        ACT(dst_fn(t0, n), acc[:, 0:n], AF.Silu, [acck], [dkeyf(t) for t in tl])

                    jobs.append((c1, c2))
                emit_skewed(jobs, 1)

            for g in range(4):
                wsb, wkey = ws_rot.next()
                load_w(wsb[:], wkey, c_win_d[j, :, 2048 + g * 512:2048 + (g + 1) * 512].rearrange("(kc p) c -> p kc c", p=128),
                       sem=("ws", wkey[1]))
                for c in range(NCH):
                    conv_chunk(wsb, wkey, c * 128, g * 4 + c, lambda t0, n, c=c: xsT[:, c, t0:t0 + n], lambda t, c=c: ("xs", c, t))
                wsb, wkey = ws_rot.next()
                sem = ("ws", wkey[1])
                load_w(wsb[:, :, 0:128], wkey, c_win_d[j, :, 4096 + g * 128:4096 + (g + 1) * 128].rearrange("(kc p) c -> p kc c", p=128), sem=sem)
                load_w(wsb[:, :, 128:256], wkey, c_win_d[j, :, 4608 + g * 128:4608 + (g + 1) * 128].rearrange("(kc p) c -> p kc c", p=128), sem=sem)
                for d in range(2):
                    c0 = 5120 + d * 32 + g * 8
                    load_w(wsb[:, :, 256 + d * 8:264 + d * 8], wkey, c_win_d[j, :, c0:c0 + 8].rearrange("(kc p) c -> p kc c", p=128), sem=sem)
                conv_chunk(wsb, wkey, 0, 16 + g, lambda t0, n: BT[:, t0:t0 + n], lambda t: ("B", t))
                conv_chunk(wsb, wkey, 128, 20 + g, lambda t0, n: CT[:, t0:t0 + n], lambda t: ("C", t))
                for t in range(NTILE):
                    pst, pkey = psrot.next()
                    for kc in range(8):
                        MM(pst[:, 0:16], hT[:, kc, t * 128:(t + 1) * 128], wsb[:, kc, 256:272], kc == 0, kc == 7,
                           [wkey, ("h", t)], [pkey])
                    for d in range(2):
                        TT(dtt[:, t, d * 8:(d + 1) * 8], pst[:, d * 8:(d + 1) * 8], dtb[:, d * 32 + g * 8:d * 32 + g * 8 + 8],
                           ALU.add, [pkey, "dtb"], [("dt", t)])
                    ACT(dtt[:, t, :], dtt[:, t, :], AF.Exp, [("dt", t)], [("dt", t)])
                    ACT(dtt[:, t, :], dtt[:, t, :], AF.Ln, [("dt", t)], [("dt", t)], bias=1.0, scale=1.0)
                wz, wzk = ws_rot.next()
                load_w(wz[:], wzk, c_win_d[j, :, g * 512:(g + 1) * 512].rearrange("(kc p) c -> p kc c", p=128), sem=("ws", wzk[1]))
                wo_, wok = ws_rot.next()
                wo = wo_[:].rearrange("p a b -> p (a b)").rearrange("p (k c) -> p k c", c=1024)
                load_w(wo, wok, c_wout_d[j, g * 512:(g + 1) * 512, :].rearrange("(kc p) c -> p kc c", p=128), sem=("ws", wok[1]))

                def scal(p, k):
                    return sm[:, p * 64 + k * 8:p * 64 + k * 8 + HP]

                def S1(T, p):
                    ptb = ps[7][:].bitcast(BF16)
                    for c in range(NCH):
                        P.add("pe", lambda e, o_=ptb[:, c * 128:(c + 1) * 128], i_=xsT[:, c, T * 128:(T + 1) * 128]:
                              e.transpose(o_, i_, ident_b[:]), [("xs", c, T), "ident_b"], [("ps", 7)])
                    P.add("pe", lambda e, o_=ptb[:, 512:640], i_=BT[:, T * 128:(T + 1) * 128]: e.transpose(o_, i_, ident_b[:]),
                          [("B", T), "ident_b"], [("ps", 7)])
                    ACT(xs_tok2[p][:], ptb[:, 0:512], AF.Copy, [("ps", 7)], [("xs_tok", p)])
                    ACT(B_tok2[p][:], ptb[:, 512:640], AF.Copy, [("ps", 7)], [("B_tok", p)])
                    MM(ps[1][:, 0:128], BT[:, T * 128:(T + 1) * 128], CT[:, T * 128:(T + 1) * 128], True, True,
                       [("B", T), ("C", T)], [("ps", 1)])
                    TT(CBm2[p][0][:], ps[1][:, 0:128], triU[:], ALU.mult, [("ps", 1), "triU"], [("CBm", p, 0)])
                    TT(CBm2[p][1][:], ps[1][:, 0:128], triL[:], ALU.mult, [("ps", 1), "triL"], [("CBm", p, 1)])

                def S2(T, d, p):
                    cum = triU if d == 0 else triL
                    ck = "triU" if d == 0 else "triL"
                    a_t, e_t, ds_t, cd_t, w2_t = (scal(p, k) for k in range(5))
                    dts = dtt[:, T, d * 8:(d + 1) * 8]
                    TT(a_t, dts, Aneg[:, d * 32 + g * 8:d * 32 + g * 8 + 8], ALU.mult, [("dt", T), "Aneg"], [("a_t", p)])
                    MM(ps[0][:, 0:HP], cum[:], a_t, True, True, [ck, ("a_t", p)], [("ps", 0)])
                    MM(ps[0][:, 64:64 + HP], ones_f[:], a_t, True, True, ["ones_f", ("a_t", p)], [("ps", 0)])
                    ACT(e_t, ps[0][:, 0:HP], AF.Exp, [("ps", 0)], [("e_t", p)])
                    ACT(cd_t, ps[0][:, 64:64 + HP], AF.Exp, [("ps", 0)], [("cd_t", p)])
                    CP(w2_t, ps[0][:, 0:HP], [("ps", 0)], [("w2_t", p)])
                    TT(ds_t, ps[0][:, 64:64 + HP], w2_t, ALU.subtract, [("ps", 0), ("w2_t", p)], [("ds_t", p)])
                    ACT(ds_t, ds_t, AF.Exp, [("ds_t", p)], [("ds_t", p)])

                def S3a(T, d, p):
                    cum, strict = (triU, triSL) if d == 0 else (triL, triSU)
                    ck, sk = ("triU", "triSL") if d == 0 else ("triL", "triSU")
                    a_t = scal(p, 0)
                    dts = dtt[:, T, d * 8:(d + 1) * 8]
                    xv = xs_tok2[p][:].rearrange("p (r q) -> p r q", q=64)
                    TT(Xdt[:].rearrange("p (r q) -> p r q", q=64), xv, dts.unsqueeze(2).to_broadcast([128, HP, 64]), ALU.mult,
                       [("xs_tok", p), ("dt", T)], ["Xdt"])
                    aus = []
                    for hq in range(HP // 4):
                        aU, aUk = tmpf_rot.next()
                        TT(aU[:].rearrange("p (r l) -> p r l", l=128), cum[:].unsqueeze(1).to_broadcast([128, 4, 128]),
                           a_t[:, hq * 4:(hq + 1) * 4].unsqueeze(2).to_broadcast([128, 4, 128]), ALU.mult, [ck, ("a_t", p)], [aUk])
                        aus.append((aU, aUk))
                    for hq in range(HP // 4):
                        MM(ps[2 + hq][:], strict[:], aus[hq][0][:], True, True, [sk, aus[hq][1]], [("ps", 2 + hq)])

                def S3b(T, d, p):
                    ds_t = scal(p, 2)
                    lms = []
                    for hq in range(HP // 4):
                        lm, lmk = tmpb_rot.next()
                        ACT(lm[:], ps[2 + hq][:], AF.Exp, [("ps", 2 + hq)], [lmk])
                        lms.append((lm, lmk))
                    for hq in range(HP // 4):
                        TT(Mt[:, hq * 4:(hq + 1) * 4, :], lms[hq][0][:].rearrange("p (r l) -> p r l", l=128),
                           CBm2[p][d][:].unsqueeze(1).to_broadcast([128, 4, 128]), ALU.mult, [lms[hq][1], ("CBm", p, d)], ["Mt"])
                    TT(Xds[:].rearrange("p (r q) -> p r q", q=64), Xdt[:].rearrange("p (r q) -> p r q", q=64),
                       ds_t.unsqueeze(2).to_broadcast([128, HP, 64]), ALU.mult, ["Xdt", ("ds_t", p)], ["Xds"])

                def S4(T, d, p):
                    for r in range(HP):
                        MM(ps[4][:, r * 64:(r + 1) * 64], Mt[:, r, :], Xdt[:, r * 64:(r + 1) * 64], r == 0, r == HP - 1,
                           ["Mt", "Xdt"], [("ps", 4)])
                    MM(ps[5][:], CT[:, T * 128:(T + 1) * 128], Sb[d][:], True, True, [("C", T), ("Sb", d)], [("ps", 5)])
                    MM(ps[6][:], B_tok2[p][:], Xds[:], True, True, [("B_tok", p), "Xds"], [("ps", 6)])

                def S5(T, d, p):
                    a_t, e_t, ds_t, cd_t, w2_t = (scal(p, k) for k in range(5))
                    yo, yok = tmpf_rot.next()
                    TT(yo[:].rearrange("p (r q) -> p r q", q=64), ps[5][:].rearrange("p (r q) -> p r q", q=64),
                       e_t.unsqueeze(2).to_broadcast([128, HP, 64]), ALU.mult, [("ps", 5), ("e_t", p)], [yok])
                    TT(yacc[:], ps[4][:], yo[:], ALU.add, [("ps", 4), yok], ["yacc"])
                    sv = Sst[d][:].rearrange("p (r q) -> p r q", q=64)
                    TT(sv, sv, cd_t.unsqueeze(2).to_broadcast([128, HP, 64]), ALU.mult, [("S", d), ("cd_t", p)], [("S", d)])
                    TT(Sst[d][:], Sst[d][:], ps[6][:], ALU.add, [("S", d), ("ps", 6)], [("S", d)])
                    ACT(Sb[d][:], Sst[d][:], AF.Copy, [("S", d)], [("Sb", d)])

                def run_pass(order, d, tail):
                    n_ = len(order)
                    P.add("dve", lambda e, ap=Sst[d][:]: e.memset(ap, 0.0), [("S", 1 - d), ("Sb", 1 - d)], [("S", d)])
                    P.add("dve", lambda e, ap=Sb[d][:]: e.memset(ap, 0.0), [("S", 1 - d), ("Sb", 1 - d)], [("Sb", d)])

                    def A(i):
                        S1(order[i], i % 2)
                        S2(order[i], d, i % 2)
                        S3a(order[i], d, i % 2)

                    A(0)
                    for i in range(n_):
                        if i > 0:
                            S5(order[i - 1], d, (i - 1) % 2)
                            tail(i - 1, order[i - 1], (i - 1) % 2)
                        S3b(order[i], d, i % 2)
                        S4(order[i], d, i % 2)
                        if i + 1 < n_:
                            A(i + 1)
                    S5(order[n_ - 1], d, (n_ - 1) % 2)
                    tail(n_ - 1, order[n_ - 1], (n_ - 1) % 2)


                def tail_f(i, T, p):
                    dx, dxk = tmpf_rot.next()
                    TT(dx[:].rearrange("p (r q) -> p r q", q=64), xs_tok2[p][:].rearrange("p (r q) -> p r q", q=64),
                       dsk[:, g * 8:(g + 1) * 8].unsqueeze(2).to_broadcast([128, HP, 64]), ALU.mult, [("xs_tok", p), "dsk"], [dxk])
                    TT(YP[:, T, :], yacc[:], dx[:], ALU.add, ["yacc", dxk], [("YP", T)])

                run_pass(list(range(NTILE)), 0, tail_f)

                border = [1, 0] + list(range(NTILE - 1, 1, -1))

                bblk = [[1, 0]] + [list(range(17 - 4 * q, 13 - 4 * q, -1)) for q in range(4)]
                blk_of = {}
                for bl in bblk:
                    for T_ in bl:
                        blk_of[T_] = bl

                def tail_b(i, T, p):
                    bl = blk_of[T]
                    t_lo = min(bl)
                    pz, pzk = ps[1], ("ps", 1)
                    for kc in range(8):
                        MM(pz[:], hT[:, kc, T * 128:(T + 1) * 128], wz[:, kc, :], kc == 0, kc == 7, [wzk, ("h", T)], [pzk])
                    zs, zsk = tmpf_rot.next()
                    ACT(zs[:], pz[:], AF.Silu, [pzk], [zsk])
                    TT(yacc[:], yacc[:], YP[:, T, :], ALU.add, ["yacc", ("YP", T)], ["yacc"])
                    TT(yacc[:], yacc[:], zs[:], ALU.mult, ["yacc", zsk], ["yacc"])
                    ss_t = sm[:, p * 64 + 40:p * 64 + 41]
                    rs_t = sm[:, p * 64 + 41:p * 64 + 42]
                    junk, jk = tmpf_rot.next()
                    P.add("dve", lambda e, o_=junk[:], i_=yacc[:], a_=ss_t: e.scalar_tensor_tensor(
                        out=o_, in0=i_, scalar=1.0, in1=i_, op0=ALU.mult, op1=ALU.mult, accum_out=a_), ["yacc"], [jk, ("ss_t", p)])
                    ACT(rs_t, ss_t, AF.Ln, [("ss_t", p)], [("rs_t", p)], scale=1.0 / NF, bias=EPS)
                    ACT(rs_t, rs_t, AF.Exp, [("rs_t", p)], [("rs_t", p)], scale=-0.5)
                    ynb, ynk = tmpb_rot.next()
                    TS(ynb[:], yacc[:], rs_t, None, ALU.mult, None, ["yacc", ("rs_t", p)], [ynk])
                    ptb = ps[7][:].bitcast(BF16)
                    for c in range(NCH):
                        P.add("pe", lambda e, o_=ptb[:, c * 128:(c + 1) * 128], i_=ynb[:, c * 128:(c + 1) * 128]:
                              e.transpose(o_, i_, ident_b[:]), [ynk, "ident_b"], [("ps", 7)])
                    off = (T - t_lo) * 128
                    for c in range(NCH):
                        TS(yT[:, c, off:off + 128], ptb[:, c * 128:(c + 1) * 128], nwT[:, g * 4 + c:g * 4 + c + 1], None,
                           ALU.mult, None, [("ps", 7), "nwT"], [("yT", T)])
                    if T != bl[-1]:
                        return
                    if t_lo == 0 and not need_ctx:
                        return
                    n = 128 * len(bl)
                    cond = nseq if t_lo == 0 else s
                    tl = list(range(t_lo, t_lo + len(bl)))
                    for dc in range(8):
                        po, pok = ps[4 + dc % 2], ("ps", 4 + dc % 2)
                        for kc in range(NCH):
                            MM(po[:, 0:n], wo[:, kc, dc * 128:(dc + 1) * 128], yT[:, kc, 0:n], kc == 0, kc == NCH - 1,
                               [wok] + [("yT", t) for t in tl], [pok])
                        rk = [("R", dc, t) for t in tl]
                        STT(R[:, dc, t_lo * 128:t_lo * 128 + n], po[:, 0:n], mod[:, li, 16 + dc, cond:cond + 1],
                            R[:, dc, t_lo * 128:t_lo * 128 + n], ALU.mult, ALU.add, [pok, ("mod", li)] + rk, rk)

                run_pass(border, 1, tail_b)

        ARENA_BYTES[0] = (int(nc.sbuf_bytes_remaining) // 64) * 64
        arena_box.append(P.sbuf("arena", [128, ARENA_BYTES[0] // 4], F32))
        stage = [carve_f(STAGE_OFF + i * 4096, D) for i in range(2)]
        stage_rot = Rot([(stage[i], ("stage", i)) for i in range(2)])
        for s in range(nseq):
            load_seq(s)
            for li, l in enumerate(layers):
                if l % 3 == 0:
                    gqa_layer(li, l, s)
                elif l % 3 == 1:
                    mla_layer(li, l, s)
                else:
                    ssd_layer(li, l, s)
                P.barrier()
            if final:
                fin = carve_f(0, 8 * 512).rearrange("p (c t) -> p c t", t=512)
                for (t0, n) in XBLK:
                    norm_block(t0, n, fgT, None, lambda dc: fin[:, dc, :], lambda dc, t: ("fin", dc, t), ["fgT"])
                    store_seq(s, fin, tiles=list(tiles_of(t0, n)), fin_t0=t0 // 128)
            else:
                store_seq(s, None)
            P.barrier()
        P.add("sp", lambda e: e.nop(), out_keys, [])
        P.emit()
        prog_stats = dict(n_ops=len(P.allops), n_waits=P.n_waits)
    return nc, prog_stats


_CONSTS = {}


def _consts():
    if _CONSTS:
        return _CONSTS
    c = _CONSTS
    c["ident"] = np.eye(128, dtype=np.float32)
    c["ones"] = np.ones((128, 128), np.float32)
    cosA, sinA, spA = rope_tables(64, 2)
    c["ropeA_cos"], c["ropeA_sin"], c["spA"] = cosA, sinA, spA
    cosB, sinB, spB = rope_tables(32, 4)
    c["ropeB_cos"], c["ropeB_sin"], c["spB"] = cosB, sinB, spB
    kj = np.arange(128)[:, None]
    qi = np.arange(128)[None, :]
    c["triU"] = np.ascontiguousarray((kj <= qi).astype(np.float32))
    c["triL"] = np.ascontiguousarray((kj >= qi).astype(np.float32))
    c["triSL"] = np.ascontiguousarray((kj > qi).astype(np.float32))
    c["triSU"] = np.ascontiguousarray((kj < qi).astype(np.float32))
    c["mask_prev"] = np.ascontiguousarray((kj >= qi).astype(np.float32))
    c["mask_next"] = np.ascontiguousarray((kj <= qi).astype(np.float32))
    return c


def _layout_params(inp):
    f = lambda a: np.ascontiguousarray(np.asarray(a, dtype=np.float32))
    p = {}
    p["ada_w"] = f(inp["ada_w"])
    p["ada_bT"] = f(np.asarray(inp["ada_b"]).reshape(DEPTH, 24, 128).transpose(0, 2, 1))
    p["norm_gT"] = f(np.asarray(inp["norm_g"]).reshape(DEPTH, 8, 128).transpose(0, 2, 1))
    p["final_gT"] = f(np.asarray(inp["final_g"]).reshape(8, 128).T)
    qperm, operm = gqa_perms()
    awin = np.asarray(inp["a_w_in"])
    awp = awin.copy()
    awp[:, :, 0:1024] = awin[:, :, qperm]
    awp[:, :, 1536:2560] = awin[:, :, 1536 + qperm]
    p["a_w_in"] = f(awp)
    p["a_w_out"] = f(np.asarray(inp["a_w_out"])[:, qperm, :])
    p["a_sink_bc"] = f(np.broadcast_to(np.asarray(inp["a_sink"])[:, None, :], (2, 128, 16)))
    uq = np.zeros(1536, np.int64)
    ukv = np.zeros(2048, np.int64)
    for c in range(8):
        A, Bh = 2 * c, 2 * c + 1
        uq[c * 192:(c + 1) * 192] = np.concatenate([A * 96 + np.arange(64), Bh * 96 + np.arange(64),
                                                    A * 96 + 64 + np.arange(32), Bh * 96 + 64 + np.arange(32)])
        ukv[c * 256:(c + 1) * 256] = np.concatenate([A * 128 + np.arange(64), Bh * 128 + np.arange(64),
                                                     A * 128 + 64 + np.arange(64), Bh * 128 + 64 + np.arange(64)])
    p["c_w_in"] = f(inp["c_w_in"])
    p["c_w_out"] = f(inp["c_w_out"])
    p["c_conv_wT"] = f(np.asarray(inp["c_conv_w"])[0].reshape(3, 24, 128).transpose(2, 1, 0))
    p["c_conv_bT"] = f(np.asarray(inp["c_conv_b"])[0].reshape(24, 128).T)
    p["c_dt_bias_bc"] = f(np.broadcast_to(np.asarray(inp["c_dt_bias"])[0].reshape(1, 64), (128, 64)))
    p["c_a_log_bc"] = f(np.broadcast_to(np.asarray(inp["c_a_log"])[0].reshape(1, 64), (128, 64)))
    p["c_d_bc"] = f(np.broadcast_to(np.asarray(inp["c_d"])[0].reshape(1, 32), (128, 32)))
    p["c_normT"] = f(np.asarray(inp["c_norm"])[0].reshape(16, 128).T)
    p["b_w_in"] = f(inp["b_w_in"])
    p["b_w_uq"] = f(inp["b_w_uq"])
    p["b_w_ukv"] = f(np.asarray(inp["b_w_ukv"])[:, :, ukv])
    p["b_w_out"] = f(inp["b_w_out"])
    p["b_q_normT"] = f(np.asarray(inp["b_q_norm"]).reshape(1, 4, 128).transpose(0, 2, 1))
    p["b_kv_normT"] = f(np.asarray(inp["b_kv_norm"]).reshape(1, 2, 128).transpose(0, 2, 1))
    return p


def _cond_T(c_rows, c_ctx):
    allc = np.concatenate([np.asarray(c_rows), np.asarray(c_ctx)[None, :]], axis=0)
    return np.ascontiguousarray(allc.reshape(-1, 8, 128).transpose(2, 1, 0).astype(np.float32))


_A_NAMES = ("a_w_in", "a_sink_bc", "a_w_out", "ropeA_cos", "ropeA_sin", "spA", "mask_prev", "mask_next")
_C_NAMES = ("c_w_in", "c_w_out", "c_conv_wT", "c_conv_bT", "c_dt_bias_bc", "c_a_log_bc", "c_d_bc", "c_normT",
            "triU", "triL", "triSL", "triSU")
_B_NAMES = ("b_w_in", "b_w_uq", "b_w_ukv", "b_w_out", "b_q_normT", "b_kv_normT", "ropeB_cos", "ropeB_sin", "spB")


def run_layers(layers, final, x, ctx, c, c_ctx, params, n_cores=N_CORES, core_ids=None):
    B = x.shape[0]
    nseq = B // n_cores
    nc, stats = build_program(layers, nseq, final)
    consts = _consts()
    in_maps = []
    for k in range(n_cores):
        sl = slice(k * nseq, (k + 1) * nseq)
        m = {"x": np.ascontiguousarray(x[sl]), "ctx": np.ascontiguousarray(ctx[sl]),
             "condT": _cond_T(c[sl], c_ctx), "ada_w": params["ada_w"], "ada_bT": params["ada_bT"],
             "norm_gT": params["norm_gT"], "final_gT": params["final_gT"],
             "ident": consts["ident"], "ones": consts["ones"]}
        if any(l % 3 == 0 for l in layers):
            for nme in _A_NAMES:
                m[nme] = params[nme] if nme in params else consts[nme]
        if any(l % 3 == 1 for l in layers):
            for nme in _B_NAMES:
                m[nme] = params[nme] if nme in params else consts[nme]
        if any(l % 3 == 2 for l in layers):
            for nme in _C_NAMES:
                m[nme] = params[nme] if nme in params else consts[nme]
        in_maps.append(m)
    res = run_bass_kernel_spmd(nc, in_maps, core_ids=list(range(n_cores)) if core_ids is None else core_ids)
    if final:
        return np.concatenate([r["out"] for r in res.results], axis=0), None
    return (np.concatenate([r["x_out"] for r in res.results], axis=0),
            np.concatenate([r["ctx_out"] for r in res.results], axis=0))


LAUNCH_PLAN = [[0, 1, 2, 3]]


def kernel(**inputs):
    inp = {k: np.asarray(v) for k, v in inputs.items()}
    params = _layout_params(inp)
    x = np.ascontiguousarray(inp["x"], dtype=np.float32)
    ctx = np.ascontiguousarray(inp["ctx"], dtype=np.float32)
    c = np.asarray(inp["c"], dtype=np.float32)
    c_ctx = np.asarray(inp["c_ctx"], dtype=np.float32)
    out = None
    for gi, layers in enumerate(LAUNCH_PLAN):
        final = gi == len(LAUNCH_PLAN) - 1
        a, b = run_layers(layers, final, x, ctx, c, c_ctx, params)
        if final:
            out = a
        else:
            x, ctx = a, b
    return out.astype(np.float32)
```
